# Optimizing a Trainium2 kernel written in Bass

```python
import jax, jax.numpy as jnp
from jax import lax
import numpy as np

D_MODEL = 1024
BATCH = 4
SEQ = 8192
DEPTH = 1

HEAD_DIM = 64
MIX_WIDTH = D_MODEL
FOX_HEADS = (MIX_WIDTH // 2) // HEAD_DIM
SWA_Q_HEADS = (MIX_WIDTH // 2) // HEAD_DIM
SWA_KV_HEADS = 2
SWA_GROUP = SWA_Q_HEADS // SWA_KV_HEADS
WINDOW = 128
Q_BLOCK = 128
ROPE_THETA = 10000.0
RMS_EPS = 1e-6
FOX_W = FOX_HEADS * HEAD_DIM
SWA_W = SWA_Q_HEADS * HEAD_DIM
SWA_KV_W = SWA_KV_HEADS * HEAD_DIM
IN_SIZES = (FOX_W, FOX_W, FOX_W, FOX_HEADS, FOX_W, SWA_W, SWA_KV_W, SWA_KV_W, SWA_W)
IN_WIDTH = FOX_W * 4 + FOX_HEADS + SWA_W * 2 + SWA_KV_W * 2

kernel_name = "hybrid_fox_swa_sink_parallel_heads"


def rms_norm(x, g):
    xf = x.astype(jnp.float32)
    y = xf * lax.rsqrt(jnp.mean(xf * xf, axis=-1, keepdims=True) + RMS_EPS)
    return (y * g.astype(jnp.float32)).astype(x.dtype)


def rope(x, positions):
    half = HEAD_DIM // 2
    inv_freq = ROPE_THETA ** (-jnp.arange(half, dtype=jnp.float32) / half)
    ang = positions.astype(jnp.float32)[..., None] * inv_freq
    cos = jnp.cos(ang)[:, :, None, :]
    sin = jnp.sin(ang)[:, :, None, :]
    xf = x.astype(jnp.float32)
    x1, x2 = xf[..., :half], xf[..., half:]
    return jnp.concatenate([x1 * cos - x2 * sin, x2 * cos + x1 * sin], axis=-1).astype(x.dtype)


def forgetting_attention(q, k, v, f_logit):
    B, S, H, d = q.shape
    nb = S // Q_BLOCK
    cum = jnp.cumsum(jax.nn.log_sigmoid(f_logit.astype(jnp.float32)), axis=1)
    cum = cum.transpose(0, 2, 1)
    kh = k.transpose(0, 2, 1, 3)
    vh = v.transpose(0, 2, 1, 3)
    q_blocks = q.transpose(0, 2, 1, 3).reshape(B, H, nb, Q_BLOCK, d).transpose(2, 0, 1, 3, 4)
    c_blocks = cum.reshape(B, H, nb, Q_BLOCK).transpose(2, 0, 1, 3)
    key_pos = jnp.arange(S)
    scale = d ** -0.5

    def one_block(args):
        qb, cb, n = args
        s = jnp.einsum('bhqd,bhkd->bhqk', qb, kh, preferred_element_type=jnp.float32) * scale
        s = s + cb[..., None] - cum[:, :, None, :]
        q_pos = n * Q_BLOCK + jnp.arange(Q_BLOCK)
        causal = key_pos[None, :] <= q_pos[:, None]
        s = jnp.where(causal, s, -jnp.inf)
        p = jax.nn.softmax(s, axis=-1)
        return jnp.einsum('bhqk,bhkd->bhqd', p.astype(vh.dtype), vh)

    out = lax.map(one_block, (q_blocks, c_blocks, jnp.arange(nb)))
    return out.transpose(1, 0, 3, 2, 4).reshape(B, S, H * d)


def sliding_window_sink_attention(q, k, v, sinks):
    B, S, _, d = q.shape
    nb = S // Q_BLOCK
    qb = q.reshape(B, nb, Q_BLOCK, SWA_KV_HEADS, SWA_GROUP, d)

    def with_prev(t):
        t = t.reshape(B, nb, Q_BLOCK, SWA_KV_HEADS, d)
        prev = jnp.pad(t[:, :-1], ((0, 0), (1, 0), (0, 0), (0, 0), (0, 0)))
        return jnp.concatenate([prev, t], axis=2)

    kk, vv = with_prev(k), with_prev(v)
    s = jnp.einsum('bnqhgd,bnshd->bnhgqs', qb, kk, preferred_element_type=jnp.float32) * (d ** -0.5)
    i = jnp.arange(Q_BLOCK)[:, None]
    j = jnp.arange(2 * Q_BLOCK)[None, :]
    rel = i + Q_BLOCK - j
    band = (rel >= 0) & (rel < WINDOW)
    valid_blk = (jnp.arange(nb)[:, None, None] > 0) | (j[None] >= Q_BLOCK)
    mask = band[None] & valid_blk
    s = jnp.where(mask[None, :, None, None], s, -jnp.inf)
    sink = jnp.broadcast_to(
        sinks.astype(jnp.float32).reshape(SWA_KV_HEADS, SWA_GROUP)[None, None, :, :, None, None],
        s.shape[:-1] + (1,))
    p = jax.nn.softmax(jnp.concatenate([s, sink], axis=-1), axis=-1)[..., :-1]
    out = jnp.einsum('bnhgqs,bnshd->bnqhgd', p.astype(vv.dtype), vv)
    return out.reshape(B, S, SWA_Q_HEADS * d)


def setup_inputs(seed: int = 0) -> dict:
    key = jax.random.key(seed)
    ks = jax.random.split(key, 12)
    x = jax.random.normal(ks[0], (BATCH, SEQ, D_MODEL), jnp.float32)
    c = jax.random.normal(ks[1], (BATCH, D_MODEL), jnp.float32)
    positions = jnp.broadcast_to(jnp.arange(SEQ, dtype=jnp.int32)[None, :], (BATCH, SEQ))
    w_ada = jax.random.normal(ks[2], (DEPTH, D_MODEL, 3 * D_MODEL), jnp.float32) * (0.5 * D_MODEL ** -0.5)
    b_ada = jax.random.normal(ks[3], (DEPTH, 3 * D_MODEL), jnp.float32) * 0.02
    g_pre = 1.0 + 0.1 * jax.random.normal(ks[4], (DEPTH, D_MODEL), jnp.float32)
    w_in = jax.random.normal(ks[5], (DEPTH, D_MODEL, IN_WIDTH), jnp.float32) * D_MODEL ** -0.5
    b_fgate = jax.random.uniform(ks[6], (DEPTH, FOX_HEADS), jnp.float32, 1.0, 4.0)
    sinks = jax.random.normal(ks[7], (DEPTH, SWA_Q_HEADS), jnp.float32)
    w_out = jax.random.normal(ks[8], (DEPTH, MIX_WIDTH, D_MODEL), jnp.float32) * MIX_WIDTH ** -0.5
    g_post = 1.0 + 0.1 * jax.random.normal(ks[9], (DEPTH, D_MODEL), jnp.float32)
    return {"x": x, "c": c, "positions": positions, "w_ada": w_ada, "b_ada": b_ada,
            "g_pre": g_pre, "w_in": w_in, "b_fgate": b_fgate, "sinks": sinks,
            "w_out": w_out, "g_post": g_post}


def reference(x, c, positions, w_ada, b_ada, g_pre, w_in, b_fgate, sinks, w_out, g_post):
    B, S, _ = x.shape
    split_points = [int(v) for v in np.cumsum(IN_SIZES)[:-1]]
    for l in range(DEPTH):
        mod = jax.nn.silu(c) @ w_ada[l] + b_ada[l]
        shift, scale, gate = jnp.split(mod, 3, axis=-1)
        h = rms_norm(x, g_pre[l]) * (1.0 + scale[:, None, :]) + shift[:, None, :]
        proj = h @ w_in[l]
        qa, ka, va, fa, za, qb, kb, vb, zb = jnp.split(proj, split_points, axis=-1)
        oa = forgetting_attention(
            qa.reshape(B, S, FOX_HEADS, HEAD_DIM),
            ka.reshape(B, S, FOX_HEADS, HEAD_DIM),
            va.reshape(B, S, FOX_HEADS, HEAD_DIM),
            fa + b_fgate[l])
        oa = oa * jax.nn.silu(za)
        qb = rope(qb.reshape(B, S, SWA_Q_HEADS, HEAD_DIM), positions)
        kb = rope(kb.reshape(B, S, SWA_KV_HEADS, HEAD_DIM), positions)
        ob = sliding_window_sink_attention(qb, kb, vb.reshape(B, S, SWA_KV_HEADS, HEAD_DIM), sinks[l])
        ob = ob * jax.nn.silu(zb)
        y = jnp.concatenate([oa, ob], axis=-1) @ w_out[l]
        x = x + gate[:, None, :] * rms_norm(y, g_post[l])
    return x
```

```python
import math
import numpy as np
import concourse.bass as bass
import concourse.mybir as mybir
from concourse.bass_utils import run_bass_kernel_spmd

F32, BF16, I32 = mybir.dt.float32, mybir.dt.bfloat16, mybir.dt.int32
ALU = mybir.AluOpType
AF = mybir.ActivationFunctionType
AX = mybir.AxisListType

S = 8192
D = 1024
T = 512
NT = S // T
HD = 64
EPS = 1e-6
NEG = -30000.0
E_TM, E_FULL, E_QK = 648, 1024, 568
NRING = 5
NY_EST = 368
YCOUNT = []

INPROJ_PIECES = (["tm%d" % i for i in range(4)] + ["zf0", "zf1", "zs0", "zs1"] +
                 ["sq%d" % i for i in range(4)] + ["sk"] +
                 ["fk%d" % i for i in range(4)] + ["fq%d" % i for i in range(4)])
PIECE_E = {}
for _n in INPROJ_PIECES:
    PIECE_E[_n] = E_TM if _n.startswith("tm") else (E_QK if _n[:2] in ("fk", "fq") else E_FULL)
PIECE_OFF = {}
_o = 0
for _n in INPROJ_PIECES:
    PIECE_OFF[_n] = _o
    _o += 128 * PIECE_E[_n]
for _k in range(8):
    PIECE_OFF["wo%d" % _k] = _o
    PIECE_E["wo%d" % _k] = E_FULL
    _o += 128 * E_FULL
W_TOTAL = _o


class Sem:
    def __init__(self, nc, name):
        self.h = nc.alloc_semaphore(name=name)
        self.n = 0


class Prog:
    ENG = ("pe", "act", "dve", "pool", "sp")

    def __init__(self, nc):
        self.nc = nc
        self.ops = {e: [] for e in self.ENG}
        self.prog = {e: Sem(nc, "pg_" + e) for e in ("pe", "act", "dve", "pool")}
        self.waited = {e: {} for e in self.ENG}

    def op(self, eng, fn, waits=(), sig=False, dsem=None, raw_inc=None):
        w = []
        for t in waits:
            if t is None:
                continue
            s, v = t
            if self.waited[eng].get(id(s), 0) >= v:
                continue
            self.waited[eng][id(s)] = v
            w.append((s.h, v))
        tok = None
        inc = None
        if raw_inc is not None:
            inc = (raw_inc.h, None)
            tok = (raw_inc, 1)
        elif dsem is not None:
            dsem.n += 16
            tok = (dsem, dsem.n)
            inc = (dsem.h, 16)
        elif sig:
            s = self.prog[eng]
            s.n += 1
            tok = (s, s.n)
            inc = (s.h, 1)
        self.ops[eng].append((w, fn, inc))
        return tok

    def run(self, eng, e):
        for (w, fn, inc) in self.ops[eng]:
            for (h, v) in w:
                e.wait_ge(h, v)
            if fn is None:
                continue
            ins = fn(e)
            if inc is not None:
                if inc[1] is None:
                    ins.then_inc(inc[0])
                else:
                    ins.then_inc(inc[0], inc[1])


_LAST_PROG = {}


def build_nc():
    nc = bass.Bass("TRN2", target_bir_lowering=False)
    dt_in = lambda name, shape, dt=F32: nc.dram_tensor(name, shape, dt, kind="ExternalInput").ap()
    x_d = dt_in("x", [S, D])
    cvec_d = dt_in("cvec", [128, 8])
    pos_d = dt_in("pos", [128, S], I32)
    wada_d = dt_in("w_ada", [24 * 128, 1024])
    badaT_d = dt_in("b_adaT", [128, 24])
    bgate_d = dt_in("b_gate_rep", [128, D])
    gpreT_d = dt_in("g_preT", [128, 8])
    gpost_d = dt_in("g_post_rep", [128, D])
    wst_d = dt_in("wstream", [W_TOTAL // 1024, 1024])
    bfg_d = dt_in("bfg_rep", [128, 4])
    sink_d = dt_in("sink_rep", [128, 4])
    invf_d = dt_in("invf", [128, 1])
    phpi_d = dt_in("phpi", [128, 1])
    y_d = nc.dram_tensor("y", [S, D], F32, kind="ExternalOutput").ap()

    wbf_t = nc.dram_tensor("wbf", [W_TOTAL // 1024, 1024], BF16)
    tab_t = nc.dram_tensor("tabd", [128, S], BF16)
    gxi_t = [nc.dram_tensor("gxi%d" % t, [512, T], BF16) for t in range(NT)]
    gxo_t = [nc.dram_tensor("gxo%d" % t, [1024, T], BF16) for t in range(NT)]
    wbf = wbf_t.ap()
    tabd = tab_t.ap()

    P = Prog(nc)
    _LAST_PROG['P'] = P
    sb = lambda name, shape, dt: nc.sbuf_tensor(name, shape, dt)
    import contextlib
    with contextlib.ExitStack() as es:
        def SB(name, shape, dt):
            return es.enter_context(nc.sbuf_tensor(name, shape, dt))

        def PS(name, shape, dt):
            return es.enter_context(nc.psum_tensor(name, shape, dt))

        KT = [SB("KT%d" % h, [128, S], BF16) for h in range(4)]
        VC = SB("VC", [128, 64, 4, 64], BF16)
        WR = SB("WR", [128, NRING, 1024], BF16)
        XB = [SB("XB%d" % i, [128, 1024], F32) for i in range(2)]
        SQ = SB("SQ", [128, 1024], F32)
        XR = SB("XR", [128, 1024], F32)
        OT = SB("OT", [128, 1024], F32)
        GG = SB("GG", [128, 1024], F32)
        XN = [SB("XN%d" % i, [128, 1024], BF16) for i in range(2)]
        HT = SB("HT", [128, 8, T], BF16)
        QT2 = [SB("QT%d" % i, [128, 4, T], BF16) for i in range(2)]
        PT = SB("PT", [128, 4, T], BF16)
        SZ2 = [SB("SZ%d" % i, [128, 4, T], BF16) for i in range(2)]
        ZE = SB("ZE", [128, T], F32)
        TAB = [SB("TAB%d" % i, [128, T], BF16) for i in range(2)]
        TT = [SB("TT%d" % i, [128, T], BF16) for i in range(2)]
        QR2 = [SB("QR%d" % i, [128, 4, T], BF16) for i in range(2)]
        KR2 = [SB("KR%d" % i, [128, 640], BF16) for i in range(2)]
        VS2 = [SB("VS%d" % i, [128, 5, 64], BF16) for i in range(2)]
        GT = SB("GT", [128, 4, T], BF16)
        GR = SB("GR", [128, 8, T], BF16)
        T1 = SB("T1", [128, T], F32)
        RB = SB("RB", [128, T], F32)
        ONB = SB("ONB", [128, 64], BF16)
        LP = SB("LP", [128, 4, 4], F32)
        AQ = SB("AQ", [128, 4, 4, 71], BF16)
        AK = SB("AK", [128, 4, 4, 71], BF16)
        SM = SB("SM", [128, 128], F32)
        ss = SM[:, 0:4]
        lnv = SM[:, 4:8]
        rstd = SM[:, 8:12]
        ssy = SM[:, 12:13]
        lny = SM[:, 13:14]
        rsy = SM[:, 14:15]
        scr = SM[:, 16:24]
        sce = SM[:, 24:32]
        sc = SM[:, 32:40]
        modT = SM[:, 40:56]
        Acol = SM[:, 56:64]
        badaT = SM[:, 64:88]
        gpreT = SM[:, 88:96]
        bfrep = SM[:, 96:100]
        sinkr = SM[:, 100:104]
        invf = SM[:, 104:105]
        phpi = SM[:, 105:106]
        sinke = SM[:, 108:112]
        ccar = SM[:, 112:116]
        U16 = SB("U16", [128, 4, 4], F32)
        E16 = SB("E16", [128, 4, 4], F32)
        L16 = SB("L16", [128, 4, 4], F32)
        CP = SB("CP", [128, 4, 4], F32)
        R1 = SB("R1", [128, 4, 4], F32)
        R2 = SB("R2", [128, 4, 4], F32)
        IDB = SB("IDB", [128, 128], BF16)
        IDF = SB("IDF", [128, 128], F32)
        UF = SB("UF", [128, 128], F32)
        EF = SB("EF", [128, 128], F32)
        ONF = SB("ONF", [128, 128], F32)
        MCUR = SB("MCUR", [128, 4, 128], BF16)
        MPRV = SB("MPRV", [128, 4, 128], BF16)
        STK = SB("STK", [128, 64], BF16)
        SPAT = SB("SPAT", [1, T], BF16)
        EPS_AP = SB("EPSAP", [128, 2], F32)

        SB_ = [PS("S%d" % i, [128, T], F32) for i in range(2)]
        OB = [PS("O%d" % i, [128, T], F32) for i in range(2)]
        PA = [PS("PA%d" % i, [128, T], F32) for i in range(3)]
        MI = PS("MI", [128, T], F32)
        TR = MI[:].bitcast(BF16)

        s_small = Sem(nc, "s_small")
        s_xb = [Sem(nc, "s_xb%d" % i) for i in range(2)]
        s_sq = Sem(nc, "s_sq")
        s_stg = [Sem(nc, "s_stg%d" % i) for i in range(2)]
        s_xr = Sem(nc, "s_xr")
        s_cast = Sem(nc, "s_cast")
        s_tabst = [Sem(nc, "s_tabst%d" % i) for i in range(2)]
        s_tab = [Sem(nc, "s_tab%d" % i) for i in range(2)]
        s_ring = [Sem(nc, "s_ring%d" % i) for i in range(NRING)]
        s_gw = Sem(nc, "s_gw")
        s_gr = Sem(nc, "s_gr")
        s_out = Sem(nc, "s_out")
        s_cc = [Sem(nc, "s_cc%d" % t) for t in range(NT)]

        def pool_chain(fns):
            tok = None
            for fn in fns:
                tok = P.op("pool", fn, waits=[tok], sig=True)
            return tok
        sel = lambda out, pattern, op, fill, base, cm: (lambda e: e.affine_select(out=out, in_=out, pattern=pattern,
                                                        compare_op=op, fill=fill, base=base, channel_multiplier=cm))
        t_sm = pool_chain([lambda e: e.memset(SM[:], 0.0)])
        t_idf = pool_chain([lambda e: e.memset(IDF[:], 0.0),
                            sel(IDF[:], [[-1, 128]], ALU.not_equal, 1.0, 0, 1),
                            lambda e: e.tensor_copy(out=IDB[:], in_=IDF[:])])
        pool_chain([lambda e: e.memset(ONF[:], 1.0)])
        pool_chain([lambda e: e.memset(EPS_AP[:, 0:1], EPS), lambda e: e.memset(EPS_AP[:, 1:2], 1.0)])
        c_a = pool_chain([lambda e: e.memset(ONB[:], 1.0)])
        CONST_A = [c_a]
        pool_chain([lambda e: e.memset(UF[:], 1.0), sel(UF[:], [[1, 128]], ALU.is_ge, 0.0, 0, -1)])
        pool_chain([lambda e: e.memset(EF[:], 1.0), sel(EF[:], [[0, 128]], ALU.is_ge, 0.0, -127, 1)])
        pool_chain([lambda e: e.memset(MCUR[:], 0.0), sel(MCUR[:], [[0, 4], [1, 128]], ALU.is_ge, NEG, 0, -1)])
        pool_chain([lambda e: e.memset(MPRV[:], 0.0), sel(MPRV[:], [[0, 4], [-1, 128]], ALU.is_ge, NEG, -1, 1)])
        pool_chain([lambda e: e.memset(STK[:], 0.0), sel(STK[:], [[-1, 64]], ALU.not_equal, 1.0, 0, 1),
                    sel(STK[:], [[-1, 64]], ALU.not_equal, 1.0, -64, 1)])
        pool_chain([lambda e: e.memset(LP[:], 0.0)])
        pool_chain([lambda e: e.memset(AQ[:], 0.0), lambda e: e.memset(AQ[:, :, :, 67:70], 8.0)])
        pool_chain([lambda e: e.memset(AK[:], 0.0), lambda e: e.memset(AK[:, :, :, 64:67], 1.0),
                    lambda e: e.memset(AK[:, :, :, 70:71], 1.0)])
        pool_chain([lambda e: e.memset(KR2[0][:], 0.0)])
        pool_chain([lambda e: e.memset(KR2[1][:], 0.0)])
        pool_chain([lambda e: e.memset(QT2[0][:], 0.0)])
        pool_chain([lambda e: e.memset(QT2[1][:], 0.0)])
        for h_ in range(4):
            c_done = pool_chain([lambda e, h_=h_: e.memset(KT[h_][:], 0.0)])
        CONST = [c_done]

        def small(dst, src):
            return P.op("sp", lambda e: e.dma_start(out=dst, in_=src), waits=[t_sm], dsem=s_small)
        small(scr, cvec_d)
        small(badaT, badaT_d)
        small(gpreT, gpreT_d)
        small(bfrep, bfg_d)
        small(sinkr, sink_d)
        small(invf, invf_d)
        small(OT[:], bgate_d)
        small(XR[:], gpost_d)
        t_small = small(phpi, phpi_d)

        rows = W_TOTAL // 1024
        nch = 4
        rpc = rows // nch
        for i in range(nch):
            t_cast = P.op("pool", lambda e, i=i: e.dma_start(out=wbf[i * rpc:(i + 1) * rpc, :],
                                                         in_=wst_d[i * rpc:(i + 1) * rpc, :]), dsem=s_cast)

        ring = {"i": 0, "free": [None] * NRING}

        def piece_view(name):
            off = PIECE_OFF[name] // 1024
            E = PIECE_E[name]
            return wbf[off:off + 128 * E // 1024, :].flatten().rearrange("(p e) -> p e", p=128)

        def get_piece(name):
            i = ring["i"]
            ring["i"] += 1
            sl = i % NRING
            E = PIECE_E[name]
            src = piece_view(name)
            tok = P.op("sp", lambda e, sl=sl, E=E, src=src: e.dma_start(out=WR[:, sl, 0:E], in_=src),
                       waits=[t_cast, ring["free"][sl]], dsem=s_ring[sl])
            return sl, tok

        def release_piece(sl, tok):
            ring["free"][sl] = tok

        t_a = P.op("act", lambda e: e.activation(out=sce, in_=scr, func=AF.Exp, scale=-1.0), waits=[t_small], sig=True)
        t_d = P.op("dve", lambda e: e.tensor_scalar_add(out=sce, in0=sce, scalar1=1.0), waits=[t_a], sig=True)
        t_d = P.op("dve", lambda e: e.reciprocal(out=sce, in_=sce), waits=[t_d], sig=True)
        t_sc = P.op("dve", lambda e: e.tensor_tensor(out=sc, in0=scr, in1=sce, op=ALU.mult), waits=[t_d], sig=True)
        SQ3 = SQ[:].rearrange("p (k n) -> p k n", k=8)
        for kc in range(8):
            t_screp = P.op("dve", lambda e, kc=kc: e.tensor_scalar(out=SQ3[:, kc, :], in0=ONF[:], scalar1=sc[:, kc:kc + 1],
                           scalar2=None, op0=ALU.mult), waits=[t_sc] + CONST_A, sig=True)
        t_a = P.op("act", lambda e: e.activation(out=sinke, in_=sinkr, func=AF.Exp), waits=[t_small], sig=True)
        for hh in range(4):
            t_spat = P.op("dve", lambda e, hh=hh: e.tensor_scalar(out=SPAT[0:1, hh * 128:(hh + 1) * 128], in0=ONF[0:1, :],
                          scalar1=sinke[0:1, hh:hh + 1], scalar2=None, op0=ALU.mult), waits=[t_a] + CONST_A, sig=True)

        xb_free = [None, None]
        t_pe_last = None
        for j in range(24):
            i = j % 2
            XB3 = XB[i][:].rearrange("p (k n) -> p k n", k=8)
            t_ld = P.op("sp", lambda e, j=j, i=i: e.dma_start(out=XB[i][:], in_=wada_d[j * 128:(j + 1) * 128, :]),
                        waits=[xb_free[i]], dsem=s_stg[i])
            for kc in range(8):
                if j < 16:
                    t_pe = P.op("pe", lambda e, j=j, kc=kc, XB3=XB3: e.matmul(MI[:, j:j + 1], lhsT=XB3[:, kc, :],
                                rhs=sc[:, kc:kc + 1], start=(kc == 0), stop=(kc == 7)),
                                waits=[t_ld, t_sc], sig=(kc == 7))
                else:
                    jj = j - 16
                    bank = PA[jj // 4]
                    c0 = (jj % 4) * 128
                    t_pe = P.op("pe", lambda e, kc=kc, XB3=XB3, bank=bank, c0=c0: e.matmul(bank[:, c0:c0 + 128],
                                lhsT=SQ3[:, kc, :], rhs=XB3[:, kc, :], start=(kc == 0), stop=(kc == 7)),
                                waits=[t_ld, t_screp], sig=(kc == 7))
            xb_free[i] = t_pe
            t_pe_last = t_pe
        ZEi = ZE[:].bitcast(I32)
        t_prev_copy = None
        t_sin = None
        t_store = [None, None]
        for ch in range(16):
            xi = ch % 2
            t_ld = P.op("act", lambda e, ch=ch: e.dma_start(out=ZEi, in_=pos_d[:, ch * 512:(ch + 1) * 512]),
                        waits=[t_prev_copy], dsem=s_sq)
            t_d = P.op("dve", lambda e: e.tensor_copy(out=T1[:], in_=ZEi), waits=[t_ld, t_sin], sig=True)
            t_d = P.op("dve", lambda e: e.tensor_scalar(out=T1[:], in0=T1[:], scalar1=invf, scalar2=phpi,
                       op0=ALU.mult, op1=ALU.add), waits=[t_d, t_small], sig=True)
            t_d = P.op("dve", lambda e: e.tensor_copy(out=ZEi, in_=T1[:]), waits=[t_d], sig=True)
            t_d = P.op("dve", lambda e: e.tensor_copy(out=RB[:], in_=ZEi), waits=[t_d], sig=True)
            t_prev_copy = t_d
            t_d = P.op("dve", lambda e: e.tensor_tensor(out=T1[:], in0=T1[:], in1=RB[:], op=ALU.subtract), waits=[t_d], sig=True)
            t_sin = P.op("act", lambda e, xi=xi: e.activation(out=XN[xi][:, 0:512], in_=T1[:], func=AF.Sin, scale=2.0 * math.pi),
                         waits=[t_d, t_store[xi]], sig=True)
            t_store[xi] = P.op("pool", lambda e, ch=ch, xi=xi: e.dma_start(out=tabd[:, ch * 512:(ch + 1) * 512], in_=XN[xi][:, 0:512]),
                               waits=[t_sin], dsem=s_tabst[xi])
        t_tab_done = t_store[1]
        t_tab_done0 = t_store[0]
        t_rope_dve = t_d

        t_d = P.op("dve", lambda e: e.tensor_tensor(out=modT, in0=MI[:, 0:16], in1=badaT[:, 0:16], op=ALU.add),
                   waits=[t_pe_last, t_small], sig=True)
        t_d = P.op("dve", lambda e: e.tensor_scalar_add(out=Acol, in0=modT[:, 8:16], scalar1=1.0), waits=[t_d], sig=True)
        t_mod = P.op("dve", lambda e: e.tensor_tensor(out=Acol, in0=Acol, in1=gpreT, op=ALU.mult), waits=[t_d], sig=True)
        SHcol = modT[:, 0:8]
        for hf in range(2):
            t_d = P.op("dve", lambda e, hf=hf: e.tensor_tensor(out=GG[:, hf * 512:(hf + 1) * 512], in0=PA[hf][:],
                       in1=OT[:, hf * 512:(hf + 1) * 512], op=ALU.add), waits=[t_pe_last, t_small], sig=True)
        t_gg = P.op("dve", lambda e: e.tensor_tensor(out=GG[:], in0=GG[:], in1=XR[:], op=ALU.mult), waits=[t_d], sig=True)
        mi_free = t_mod
        pa_free = [t_gg, t_gg, None]

        st = dict(
            n=0, qk=[], ex=[], pv=[],
            om=0, o_free=[None, None],
            pa_i=0, pa_free=pa_free,
            mi_free=mi_free,
            tr_free=None,
            ht_free=None,
            xn_free=[t_tab_done0, t_tab_done],
            sq_free=t_pe_last, xr_free=t_gg, ot_free=t_gg,
            aux_free=None,
            x_scaled=[xb_free[0], xb_free[1]],
            gt_free=None, gr_free=None, ze_free=t_rope_dve, rb_free=t_rope_dve,
            pending=[],
        )
        TS = {}
        gtw = []

        def acquire_pa():
            i = st["pa_i"] % 3
            st["pa_i"] += 1
            return i, PA[i], st["pa_free"][i]

        def phase_a(t):
            toks = []
            TS.setdefault(t, {})["ht"] = toks
            for nb in range(4):
                gb = 4 * t + nb
                i = gb % 2
                r0 = gb * 128
                t_ld = P.op("pool", lambda e, i=i, r0=r0: e.dma_start(out=XB[i][:], in_=x_d[r0:r0 + 128, :]),
                            waits=[st["x_scaled"][i]], dsem=s_xb[i])
                t_p = P.op("pool", lambda e, i=i: e.tensor_tensor(out=SQ[:], in0=XB[i][:], in1=XB[i][:], op=ALU.mult),
                           waits=[t_ld, st["sq_free"]], sig=True)
                st["sq_free"] = None
                yield
                yield
                yield
                t_d = P.op("dve", lambda e, nb=nb: e.tensor_reduce(out=ss[:, nb:nb + 1], in_=SQ[:], axis=AX.X, op=ALU.add),
                           waits=[t_p], sig=True)
                st["sq_free"] = t_d
                for _ in range(8):
                    yield
                t_a = P.op("act", lambda e, nb=nb: e.activation(out=lnv[:, nb:nb + 1], in_=ss[:, nb:nb + 1], func=AF.Ln,
                           scale=1.0 / D, bias=EPS_AP[:, 0:1]), waits=[t_d], sig=True)
                t_a = P.op("act", lambda e, nb=nb: e.activation(out=rstd[:, nb:nb + 1], in_=lnv[:, nb:nb + 1], func=AF.Exp,
                           scale=-0.5), waits=[t_a], sig=True)
                yield
                yield
                t_xn = P.op("dve", lambda e, i=i, nb=nb: e.tensor_scalar(out=XN[i][:], in0=XB[i][:], scalar1=rstd[:, nb:nb + 1],
                            scalar2=None, op0=ALU.mult), waits=[t_a, t_ld, st["xn_free"][i]], sig=True)
                st["x_scaled"][i] = t_xn
                yield
                for fc in range(8):
                    t_pe = P.op("pe", lambda e, i=i, fc=fc: e.transpose(TR[:, fc * 128:(fc + 1) * 128],
                                XN[i][:, fc * 128:(fc + 1) * 128], IDB[:]),
                                waits=[t_xn, st["mi_free"]] + CONST, sig=(fc == 7))
                    if fc % 2 == 1 and fc < 7:
                        yield
                st["xn_free"][i] = t_pe
                yield
                yield
                for fc in range(8):
                    t_e = P.op("dve", lambda e, fc=fc, nb=nb: e.tensor_scalar(out=HT[:, fc, nb * 128:(nb + 1) * 128],
                               in0=TR[:, fc * 128:(fc + 1) * 128], scalar1=Acol[:, fc:fc + 1], scalar2=SHcol[:, fc:fc + 1],
                               op0=ALU.mult, op1=ALU.add), waits=[t_pe, st["ht_free"], t_mod], sig=(fc == 7))
                st["mi_free"] = t_e
                toks.append(t_e)
                yield

        def phase_b(t):
            ts = TS.setdefault(t, {})
            ht_ready = ts["ht"]
            par = t % 2
            prev = TS.get(t - 1, {})
            prev2 = TS.get(t - 2, {})
            QTp, SZp, QRp, KRp, VSp = QT2[par], SZ2[par], QR2[par], KR2[par], VS2[par]
            t_tab = P.op("pool", lambda e: e.dma_start(out=TAB[par][:], in_=tabd[:, t * T:(t + 1) * T]),
                         waits=[t_tab_done, t_tab_done0, prev2.get("tab_free")], dsem=s_tab[par])
            carry_tok = []
            if t > 0:
                KRq, VSq = KR2[1 - par], VS2[1 - par]
                tk = P.op("dve", lambda e: e.tensor_copy(out=KRp[0:64, 0:128], in_=KRq[0:64, 512:640]),
                          waits=[prev.get("kr_tok"), prev2.get("swa_qk_last")], sig=True)
                tv = P.op("dve", lambda e: e.tensor_copy(out=VSp[:, 0, :], in_=VSq[:, 4, :]),
                          waits=[prev.get("vs_tok"), prev2.get("swa_pv_last")], sig=True)
                carry_tok = [tk, tv]
            swa_free = [prev2.get("swa_qk_last"), prev2.get("swa_pv_last")]
            tm = [get_piece("tm%d" % i) for i in range(4)]
            t_u_last = None
            vc_tok = []
            for nb in range(4):
                pi, bank, bfree = acquire_pa()
                for kc in range(8):
                    sl, ltok = tm[kc // 2]
                    c0 = (kc % 2) * 324
                    t_pe = P.op("pe", lambda e, bank=bank, kc=kc, nb=nb, sl=sl, c0=c0: e.matmul(bank[:, 0:324],
                                lhsT=HT[:, kc, nb * 128:(nb + 1) * 128], rhs=WR[:, sl, c0:c0 + 324],
                                start=(kc == 0), stop=(kc == 7)),
                                waits=[ltok, bfree, ht_ready[nb]], sig=(kc == 7))
                    if kc % 2 == 1 and kc < 7:
                        yield
                gb = 4 * t + nb
                yield
                t1 = P.op("dve", lambda e, bank=bank, gb=gb: e.tensor_copy(out=VC[:, gb, :, :],
                          in_=bank[:, 0:256].rearrange("p (h d) -> p h d", h=4)), waits=[t_pe], sig=True)
                t2 = P.op("dve", lambda e, bank=bank, nb=nb: e.tensor_tensor(out=U16[:, nb, :], in0=bank[:, 256:260],
                          in1=bfrep, op=ALU.add), waits=[t_pe, t_small, st.get("u_free")], sig=True)
                t3 = P.op("dve", lambda e, bank=bank, nb=nb: e.tensor_copy(out=VSp[:, 1 + nb, :], in_=bank[:, 260:324]),
                          waits=[t_pe] + carry_tok + swa_free, sig=True)
                st["pa_free"][pi] = t3
                t_u_last = t2
                vc_tok.append(t1)
                ts["vs_tok"] = t3
                yield
            for (sl, ltok) in tm:
                release_piece(sl, t_pe)
            ts["vc_tok"] = vc_tok
            yield
            t_a = P.op("act", lambda e: e.activation(out=E16[:], in_=U16[:], func=AF.Exp, scale=-1.0), waits=[t_u_last], sig=True)
            t_l = P.op("act", lambda e: e.activation(out=L16[:], in_=E16[:], func=AF.Ln, bias=EPS_AP[:, 1:2]),
                       waits=[t_a, st.get("l_free")], sig=True)
            st["u_free"] = t_a
            yield
            t_lp = P.op("dve", lambda e: e.tensor_copy(out=LP[:, 1, :], in_=L16[:, 0, :]), waits=[t_l, st.get("l_free")], sig=True)
            t_lp = P.op("dve", lambda e: e.tensor_tensor(out=LP[:, 2, :], in0=LP[:, 1, :], in1=L16[:, 1, :], op=ALU.add), waits=[t_lp], sig=True)
            t_lp = P.op("dve", lambda e: e.tensor_tensor(out=LP[:, 3, :], in0=LP[:, 2, :], in1=L16[:, 2, :], op=ALU.add), waits=[t_lp], sig=True)
            MI16 = MI[:, 0:16]
            P.op("pe", lambda e: e.matmul(MI16, lhsT=UF[:], rhs=L16[:].rearrange("p a b -> p (a b)"), start=True, stop=False),
                 waits=[t_l, st["mi_free"], st.get("ccar_tok")] + CONST)
            P.op("pe", lambda e: e.matmul(MI16, lhsT=ONF[:], rhs=LP[:].rearrange("p a b -> p (a b)"), start=False, stop=False),
                 waits=[t_lp])
            for nb in range(4):
                t_pe = P.op("pe", lambda e, nb=nb: e.matmul(MI[:, nb * 4:(nb + 1) * 4], lhsT=EF[:], rhs=ccar, start=False, stop=(nb == 3)),
                            sig=(nb == 3))
            yield
            st["l_free"] = t_pe
            yield
            t_c = P.op("dve", lambda e: e.tensor_copy(out=CP[:], in_=MI[:, 0:16].rearrange("p (a b) -> p a b", a=4)),
                       waits=[t_pe], sig=True)
            st["mi_free"] = t_c
            st["ccar_tok"] = P.op("dve", lambda e: e.tensor_copy(out=ccar, in_=CP[:, 3, :]), waits=[t_c], sig=True)
            afree = st["aux_free"]
            t_d = P.op("dve", lambda e: e.tensor_copy(out=AK[:, :, :, 67], in_=CP[:]), waits=[t_c, afree], sig=True)
            t_d = P.op("dve", lambda e: e.tensor_tensor(out=R1[:], in0=CP[:], in1=AK[:, :, :, 67], op=ALU.subtract), waits=[t_d], sig=True)
            t_d = P.op("dve", lambda e: e.tensor_copy(out=AK[:, :, :, 68], in_=R1[:]), waits=[t_d], sig=True)
            t_d = P.op("dve", lambda e: e.tensor_tensor(out=R2[:], in0=R1[:], in1=AK[:, :, :, 68], op=ALU.subtract), waits=[t_d], sig=True)
            t_d = P.op("dve", lambda e: e.tensor_copy(out=AK[:, :, :, 69], in_=R2[:]), waits=[t_d], sig=True)
            t_aux = P.op("dve", lambda e: e.tensor_scalar(out=AQ[:, :, :, 64:67], in0=AK[:, :, :, 67:70], scalar1=-8.0,
                         scalar2=None, op0=ALU.mult), waits=[t_d], sig=True)
            yield

            def fm_group(pname, M, res, aux=None):
                sl, ltok = get_piece(pname)
                pi, bank, bfree = acquire_pa()
                E = PIECE_E[pname]
                Wv = WR[:, sl, 0:E].rearrange("p (k m) -> p k m", k=8)
                for kc in range(8):
                    t_pe = P.op("pe", lambda e, bank=bank, kc=kc, Wv=Wv, M=M: e.matmul(bank[0:M, :], lhsT=Wv[:, kc, 0:M],
                                rhs=HT[:, kc, :], start=(kc == 0), stop=(kc == 7 and aux is None)),
                                waits=[ltok, bfree] + ht_ready, sig=(kc == 7))
                    if kc % 2 == 1 and kc < 7:
                        yield
                release_piece(sl, t_pe)
                if aux is not None:
                    A_, h = aux
                    for nb in range(4):
                        t_pe = P.op("pe", lambda e, bank=bank, nb=nb, A_=A_, h=h: e.matmul(bank[0:71, nb * 128:(nb + 1) * 128],
                                    lhsT=A_[:, nb, h, :], rhs=IDB[:], start=False, stop=(nb == 3)),
                                    waits=[t_aux], sig=(nb == 3))
                res["pi"], res["bank"], res["t_pe"] = pi, bank, t_pe

            sz_free = prev2.get("g_last")
            ts["sz_ready"] = [None] * 4
            for ci, pname in enumerate(["zf0", "zf1", "zs0", "zs1"]):
                r = {}
                yield from fm_group(pname, 128, r)
                pi, bank, t_pe = r["pi"], r["bank"], r["t_pe"]
                yield
                yield
                t_a = P.op("act", lambda e, bank=bank: e.activation(out=ZE[:], in_=bank[:], func=AF.Exp, scale=-1.0),
                           waits=[t_pe, st.get("ze_free")], sig=True)
                t_a = P.op("act", lambda e: e.activation(out=ZE[:], in_=ZE[:], func=AF.Ln, bias=EPS_AP[:, 1:2]), waits=[t_a], sig=True)
                t_a = P.op("act", lambda e: e.activation(out=ZE[:], in_=ZE[:], func=AF.Exp, scale=-1.0), waits=[t_a], sig=True)
                yield
                yield
                t_d = P.op("dve", lambda e, bank=bank, ci=ci: e.tensor_tensor(out=SZp[:, ci, :], in0=bank[:], in1=ZE[:], op=ALU.mult),
                           waits=[t_a, sz_free], sig=True)
                st["ze_free"] = t_d
                st["pa_free"][pi] = t_d
                ts["sz_ready"][ci] = t_d
                yield
            for idx, pname in enumerate(["sq0", "sq1", "sq2", "sq3", "sk"]):
                r = {}
                yield from fm_group(pname, 128, r)
                pi, bank, t_pe = r["pi"], r["bank"], r["t_pe"]
                j = idx % 2
                yield
                t_d = P.op("dve", lambda e, bank=bank, j=j: e.tensor_tensor(out=TT[j][:], in0=bank[:], in1=TAB[par][:], op=ALU.mult),
                           waits=[t_pe, t_tab, st.get("tt_free%d" % j)], sig=True)
                st["pa_free"][pi] = t_d
                t_p2 = P.op("pe", lambda e, j=j: e.matmul(MI[0:64, :], lhsT=STK[:], rhs=TT[j][:], start=True, stop=True),
                            waits=[t_d, st["mi_free"]] + CONST, sig=True)
                st["tt_free%d" % j] = t_p2
                yield
                if idx < 4:
                    t_e = P.op("dve", lambda e, idx=idx: e.tensor_copy(out=QRp[0:64, idx, :], in_=MI[0:64, :]),
                               waits=[t_p2] + swa_free, sig=True)
                    ts["qr_tok"] = t_e
                else:
                    t_e = P.op("dve", lambda e: e.tensor_copy(out=KRp[0:64, 128:640], in_=MI[0:64, :]),
                               waits=[t_p2] + carry_tok + swa_free, sig=True)
                    ts["kr_tok"] = t_e
                st["mi_free"] = t_e
                ts["tab_free"] = t_d
                yield
            kt_tok = []
            for h in range(4):
                r = {}
                yield from fm_group("fk%d" % h, 71, r, aux=(AK, h))
                pi, bank, t_pe = r["pi"], r["bank"], r["t_pe"]
                yield
                t_e = P.op("dve", lambda e, bank=bank, h=h: e.tensor_copy(out=KT[h][0:71, t * T:(t + 1) * T], in_=bank[0:71, :]),
                           waits=[t_pe], sig=True)
                st["pa_free"][pi] = t_e
                kt_tok.append(t_e)
                yield
            ts["kt_tok"] = kt_tok
            qt_tok = []
            for h in range(4):
                r = {}
                yield from fm_group("fq%d" % h, 71, r, aux=(AQ, h))
                pi, bank, t_pe = r["pi"], r["bank"], r["t_pe"]
                yield
                t_e = P.op("dve", lambda e, bank=bank, h=h: e.tensor_copy(out=QTp[0:71, h, :], in_=bank[0:71, :]),
                           waits=[t_pe, prev2.get("qt_last")], sig=True)
                st["pa_free"][pi] = t_e
                qt_tok.append(t_e)
                yield
            ts["qt_tok"] = qt_tok
            st["aux_free"] = t_pe
            st["ht_free"] = t_pe

        def flush_pending(force=False):
            keep = []
            items = st["pending"]
            st["pending"] = []
            for item in items:
                item[0] -= 1
                if item[0] <= 0 or force:
                    item[1]()
                else:
                    keep.append(item)
            st["pending"] = keep + st["pending"]

        def attention(t, bg, bg_steps):
            ts = TS[t]
            par = t % 2
            QTp, SZp, QRp, KRp, VSp = QT2[par], SZ2[par], QR2[par], KR2[par], VS2[par]
            vc_tok, kt_tok, qt_tok = ts["vc_tok"], ts["kt_tok"], ts["qt_tok"]
            tiles = []
            for h in range(4):
                nkb = 4 * t + 4
                for kb in range(nkb):
                    j = kb - 4 * t
                    tiles.append(dict(kind="fox", h=h, kb=kb, c0=(128 * j if j >= 0 else 0), diag=(j >= 0),
                                      first=(kb == 0), last=(kb == nkb - 1)))
            for nb in range(4):
                seq = []
                if not (t == 0 and nb == 0):
                    seq.append("prev")
                seq.append("cur")
                for k, which in enumerate(seq):
                    tiles.append(dict(kind="swa", nb=nb, which=which, first=(k == 0), last=(k == len(seq) - 1)))

            def emit_qk(tl):
                n = st["n"]
                tl["n"] = n
                st["n"] += 1
                Sb = SB_[n % 2]
                sfree = st["ex"][n - 2] if n >= 2 else None
                if tl["kind"] == "fox":
                    h, kb, c0 = tl["h"], tl["kb"], tl["c0"]
                    w = [sfree, qt_tok[h], kt_tok[h]]
                    tq = P.op("pe", lambda e: e.matmul(Sb[:, c0:T], lhsT=KT[h][:, kb * 128:(kb + 1) * 128],
                              rhs=QTp[:, h, c0:T], start=True, stop=(not tl["diag"])), waits=w + CONST, sig=(not tl["diag"]))
                    if tl["diag"]:
                        tq = P.op("pe", lambda e: e.matmul(Sb[:, c0:c0 + 128], lhsT=IDB[:], rhs=MCUR[:, 0, :],
                                  start=False, stop=True), waits=CONST, sig=True)
                    ts["qt_last"] = tq
                else:
                    nb, which = tl["nb"], tl["which"]
                    off = nb * 128 if which == "prev" else (nb + 1) * 128
                    Sv = Sb[:].rearrange("p (h q) -> p h q", h=4)
                    M_ = MPRV if which == "prev" else MCUR
                    w = [sfree, ts["qr_tok"], ts["kr_tok"]]
                    P.op("pe", lambda e: e.matmul(Sv, lhsT=KRp[0:64, off:off + 128], rhs=QRp[0:64, :, nb * 128:(nb + 1) * 128],
                         start=True, stop=False), waits=w)
                    tq = P.op("pe", lambda e: e.matmul(Sv, lhsT=IDB[:], rhs=M_[:], start=False, stop=True), waits=CONST, sig=True)
                    ts["swa_qk_last"] = tq
                st["qk"].append(tq)
                c0 = tl.get("c0", 0)
                pfree = st["pv"][n - 4] if n >= 4 else None
                te = P.op("act", lambda e: e.activation(out=PT[:, n % 4, c0:T], in_=Sb[:, c0:T], func=AF.Exp, scale=0.125),
                          waits=[tq, pfree], sig=True)
                st["ex"].append(te)

            def emit_pv(tl):
                n = tl["n"]
                c0 = tl.get("c0", 0)
                if tl["first"]:
                    st["cur_ob"] = st["om"] % 2
                    st["om"] += 1
                    fin_prev = st.setdefault("o_fin", [None, None])[st["cur_ob"]]
                    if fin_prev is not None:
                        fin_prev()
                ob = st["cur_ob"]
                Ob = OB[ob]
                w = [st["ex"][n]]
                if tl["first"]:
                    w.append(st["o_free"][ob])
                if tl["kind"] == "fox":
                    h, kb = tl["h"], tl["kb"]
                    w.append(vc_tok[min(3, max(0, kb - 4 * t))])
                    P.op("pe", lambda e: e.matmul(Ob[0:64, c0:T], lhsT=VC[:, kb, h, :], rhs=PT[:, n % 4, c0:T],
                         start=tl["first"], stop=tl["last"]), waits=w)
                    tp = P.op("pe", lambda e: e.matmul(Ob[64:128, c0:T], lhsT=ONB[:], rhs=PT[:, n % 4, c0:T],
                              start=tl["first"], stop=tl["last"]), waits=CONST, sig=True)
                    st["pv"].append(tp)
                    if tl["last"]:
                        head_epilogue(ob, tp, fox_h=h)
                else:
                    nb, which = tl["nb"], tl["which"]
                    slot = nb if which == "prev" else nb + 1
                    w.append(ts["vs_tok"])
                    P.op("pe", lambda e: e.matmul(Ob[0:64, :], lhsT=VSp[:, slot, :], rhs=PT[:, n % 4, :],
                         start=tl["first"], stop=tl["last"]), waits=w)
                    tp = P.op("pe", lambda e: e.matmul(Ob[64:128, :], lhsT=ONB[:], rhs=PT[:, n % 4, :],
                              start=tl["first"], stop=False), waits=CONST, sig=True)
                    st["pv"].append(tp)
                    if tl["last"]:
                        tp2 = P.op("pe", lambda e: e.matmul(Ob[64:128, :], lhsT=ONB[0:1, :], rhs=SPAT[0:1, :], start=False, stop=True),
                                   waits=[t_spat] + CONST, sig=True)
                        ts["swa_pv_last"] = tp2
                        head_epilogue(ob, tp2, swa_nb=nb)

            def head_epilogue(ob, tp, fox_h=None, swa_nb=None):
                Ob = OB[ob]
                H = {"s1": False, "s2": None}
                rq_free = st.setdefault("rq_free", {})
                rq_fin = st.setdefault("rq_fin", {})
                if fox_h is not None:
                    r0f = (fox_h % 2) * 64
                    quads = [(fox_h % 2, 0), (fox_h % 2, 1)]
                    parts = [(r0f, Ob[64:128, :], RB[r0f:r0f + 64, :])]
                else:
                    cb = (swa_nb % 2) * 256
                    quads = [(0, swa_nb % 2), (1, swa_nb % 2)]
                    Ov = Ob[64:128, :].rearrange("p (a b q) -> p a b q", a=2, b=2)
                    Rv = [RB[r * 64:(r + 1) * 64, cb:cb + 256].rearrange("p (a q) -> p a q", a=2) for r in range(2)]
                    Tv = [T1[r * 64:(r + 1) * 64, cb:cb + 256].rearrange("p (a q) -> p a q", a=2) for r in range(2)]
                    parts = [(0, Ov[:, :, 0, :], Rv[0]), (64, Ov[:, :, 1, :], Rv[1])]

                prev_fins = [rq_fin.get(q) for q in quads]

                def stage1():
                    if H["s1"]:
                        return
                    H["s1"] = True
                    for f in prev_fins:
                        if f is not None:
                            f()
                    t_rs = []
                    for (r0, src, dst) in parts:
                        t_a = P.op("act", lambda e, src=src, dst=dst: e.activation(out=dst, in_=src, func=AF.Ln),
                                   waits=[tp] + [rq_free.get(q, st.get("rb_free")) for q in quads], sig=True)
                        t_r = P.op("act", lambda e, dst=dst: e.activation(out=dst, in_=dst, func=AF.Exp, scale=-1.0),
                                   waits=[t_a], sig=True)
                        t_rs.append(t_r)

                    def stage2_body():
                        if fox_h is not None:
                            h = fox_h
                            r0 = (h % 2) * 64
                            ci = h // 2
                            t_1 = P.op("dve", lambda e: e.tensor_tensor(out=T1[r0:r0 + 64, :], in0=Ob[0:64, :], in1=RB[r0:r0 + 64, :], op=ALU.mult),
                                       waits=[t_rs[0]], sig=True)
                            t_g = P.op("dve", lambda e: e.tensor_tensor(out=GT[r0:r0 + 64, ci, :], in0=T1[r0:r0 + 64, :], in1=SZp[r0:r0 + 64, ci, :], op=ALU.mult),
                                       waits=[t_1, st["gt_free"], ts["sz_ready"][ci]], sig=True)
                            t_o = t_1
                        else:
                            nb = swa_nb
                            O4 = Ob[0:64, :].rearrange("p (a b q) -> p a b q", a=2, b=2)
                            t_g = None
                            for par_ in range(2):
                                r0 = par_ * 64
                                t_1 = P.op("dve", lambda e, par_=par_: e.tensor_tensor(out=Tv[par_], in0=O4[:, :, par_, :],
                                           in1=Rv[par_], op=ALU.mult), waits=[t_rs[par_], t_g], sig=True)
                                t_g = P.op("dve", lambda e, r0=r0, par_=par_: e.tensor_tensor(out=GT[r0:r0 + 64, 2:4, nb * 128:(nb + 1) * 128],
                                           in0=Tv[par_], in1=SZp[r0:r0 + 64, 2:4, nb * 128:(nb + 1) * 128], op=ALU.mult),
                                           waits=[t_1, st["gt_free"], ts["sz_ready"][2], ts["sz_ready"][3]], sig=True)
                            t_o = t_1
                        for q in quads:
                            rq_free[q] = t_g
                        st["o_free"][ob] = t_o
                        ts["g_last"] = t_g
                        gtw.append(t_g)
                    done = {"d": False}

                    def stage2():
                        if not done["d"]:
                            done["d"] = True
                            stage2_body()
                    H["s2"] = stage2
                    st["pending"].append([6 if fox_h is not None else 4, stage2])

                def fin():
                    stage1()
                    H["s2"]()
                for q in quads:
                    rq_fin[q] = fin
                st.setdefault("o_fin", [None, None])[ob] = fin
                st["pending"].append([3, stage1])

            acc = 0.0
            for i, tl in enumerate(tiles):
                emit_qk(tl)
                acc += bg_steps
                while acc >= 1.0:
                    if next(bg, "END") != "END":
                        st["bg_used"] = st.get("bg_used", 0) + 1
                    acc -= 1.0
                if i >= 1:
                    emit_pv(tiles[i - 1])
                flush_pending()
            emit_pv(tiles[-1])
            for _ in range(4):
                flush_pending(force=True)
            nleft = 0
            for _ in bg:
                nleft += 1
            YCOUNT.append((t, st.get("bg_used", 0), nleft))
            st["bg_used"] = 0

        def exchange(t):
            toks = list(gtw)
            del gtw[:]
            gi = gxi_t[t].ap()
            go = gxo_t[t].ap()
            t_w = P.op("pool", lambda e: e.dma_start(out=gi.rearrange("(c p) n -> p c n", p=128), in_=GT[:]),
                       waits=toks, dsem=s_gw)
            st["gt_free"] = t_w
            t_c = P.op("pool", lambda e: e.collective_compute("AllGather", ALU.bypass,
                       replica_groups=[[0, 1], [2, 3], [4, 5], [6, 7]], ins=[gi.opt()], outs=[go.opt()]),
                       waits=[t_w], raw_inc=s_cc[t])
            return t_c

        def epilogue(t, t_c, lead=0):
            for _ in range(lead):
                yield
            go = gxo_t[t].ap()
            gov = go.rearrange("(c p) n -> p c n", p=128)
            t_gr = P.op("pool", lambda e: e.dma_start(out=GR[:], in_=gov), waits=[t_c, st.get("gr_free")], dsem=s_gr)
            for nb in range(4):
                r0 = t * T + nb * 128
                p0, b0, f0 = acquire_pa()
                p1, b1, f1 = acquire_pa()
                banks = [b0, b1]
                for kc in range(8):
                    sl, ltok = get_piece("wo%d" % kc)
                    for hf in range(2):
                        t_pe = P.op("pe", lambda e, kc=kc, hf=hf, sl=sl, nb=nb, banks=banks: e.matmul(banks[hf][:],
                                    lhsT=GR[:, kc, nb * 128:(nb + 1) * 128], rhs=WR[:, sl, hf * 512:(hf + 1) * 512],
                                    start=(kc == 0), stop=(kc == 7)), waits=[ltok, t_gr, f0, f1], sig=(hf == 1))
                    release_piece(sl, t_pe)
                    yield
                st["gr_free"] = t_pe
                yield
                for hf in range(2):
                    t_y = P.op("dve", lambda e, hf=hf, banks=banks: e.tensor_copy(out=OT[:, hf * 512:(hf + 1) * 512], in_=banks[hf][:]),
                               waits=[t_pe, st["ot_free"]], sig=True)
                st["pa_free"][p0] = t_y
                st["pa_free"][p1] = t_y
                yield
                t_p = P.op("pool", lambda e: e.tensor_tensor(out=XR[:], in0=OT[:], in1=OT[:], op=ALU.mult),
                           waits=[t_y, st["xr_free"]], sig=True)
                yield
                yield
                yield
                t_d = P.op("dve", lambda e: e.tensor_reduce(out=ssy, in_=XR[:], axis=AX.X, op=ALU.add), waits=[t_p], sig=True)
                t_x = P.op("pool", lambda e, r0=r0: e.dma_start(out=XR[:], in_=x_d[r0:r0 + 128, :]),
                           waits=[t_d], dsem=s_xr)
                for _ in range(8):
                    yield
                t_a = P.op("act", lambda e: e.activation(out=lny, in_=ssy, func=AF.Ln, scale=1.0 / D, bias=EPS_AP[:, 0:1]),
                           waits=[t_d], sig=True)
                t_a = P.op("act", lambda e: e.activation(out=rsy, in_=lny, func=AF.Exp, scale=-0.5), waits=[t_a], sig=True)
                yield
                t_d = P.op("dve", lambda e: e.scalar_tensor_tensor(out=OT[:], in0=OT[:], scalar=rsy, in1=GG[:],
                           op0=ALU.mult, op1=ALU.mult), waits=[t_a, t_gg], sig=True)
                yield
                t_p = P.op("pool", lambda e: e.tensor_tensor(out=XR[:], in0=OT[:], in1=XR[:], op=ALU.add),
                           waits=[t_d, t_x], sig=True)
                st["ot_free"] = t_p
                t_o = P.op("pool", lambda e, r0=r0: e.dma_start(out=y_d[r0:r0 + 128, :], in_=XR[:]), waits=[t_p], dsem=s_out)
                st["xr_free"] = t_o
                st["t_out"] = t_o
                yield

        import itertools
        for _ in phase_a(0):
            pass
        for _ in phase_b(0):
            pass
        cc_tok = {}
        for t in range(NT):
            par_gens = []
            if t + 1 < NT:
                par_gens.append(phase_a(t + 1))
            if t >= 1:
                par_gens.append(epilogue(t - 1, cc_tok[t - 1], lead=90))
            def rr(gs):
                gs = list(gs)
                while gs:
                    for g_ in list(gs):
                        try:
                            next(g_)
                            yield
                        except StopIteration:
                            gs.remove(g_)
            bg = itertools.chain(rr(par_gens), phase_b(t + 1) if t + 1 < NT else iter(()))
            ntiles = 16 * (t + 1) + 8
            bg_steps = float(NY_EST) / ntiles
            attention(t, bg, bg_steps)
            cc_tok[t] = exchange(t)
        for _ in epilogue(NT - 1, cc_tok[NT - 1]):
            pass
        P.op("pool", None, waits=[st["t_out"]])

        with nc.Block() as block:
            @block.tensor
            def _(e):
                P.run("pe", e)

            @block.scalar
            def _(e):
                P.run("act", e)

            @block.vector
            def _(e):
                P.run("dve", e)

            @block.gpsimd
            def _(e):
                P.run("pool", e)

            @block.sync
            def _(e):
                P.run("sp", e)
    return nc


IN_OFF = {"fq": 0, "fk": 512, "fv": 1024, "f": 1536, "fz": 1544, "sq": 2056, "sk": 2568, "sv": 2696, "sz": 2824}


def _pieces_for_group(w_in, w_out, g):
    def cols(base, lo, n):
        return w_in[:, base + lo: base + lo + n]
    out = []
    tmw = np.concatenate([cols(IN_OFF["fv"], 256 * g, 256), cols(IN_OFF["f"], 4 * g, 4), cols(IN_OFF["sv"], 64 * g, 64)], axis=1)
    tm4 = tmw.reshape(8, 128, 324)
    for i in range(4):
        out.append(np.ascontiguousarray(tm4[2 * i:2 * i + 2].transpose(1, 0, 2)).reshape(128, 648))

    def fm(wc):
        M = wc.shape[1]
        return np.ascontiguousarray(wc.reshape(8, 128, M).transpose(1, 0, 2)).reshape(128, 8 * M)
    for ci in range(2):
        out.append(fm(cols(IN_OFF["fz"], 256 * g + 128 * ci, 128)))
    for ci in range(2):
        out.append(fm(cols(IN_OFF["sz"], 256 * g + 128 * ci, 128)))
    perm = np.concatenate([np.arange(32, 64), np.arange(0, 32)])
    for h in range(4):
        q = cols(IN_OFF["sq"], 256 * g + 64 * h, 64)
        out.append(fm(np.concatenate([q, q[:, perm]], axis=1)))
    k = cols(IN_OFF["sk"], 64 * g, 64)
    out.append(fm(np.concatenate([k, k[:, perm]], axis=1)))
    z7 = np.zeros((1024, 7), np.float32)
    for h in range(4):
        out.append(fm(np.concatenate([cols(IN_OFF["fk"], 256 * g + 64 * h, 64), z7], axis=1)))
    for h in range(4):
        out.append(fm(np.concatenate([cols(IN_OFF["fq"], 256 * g + 64 * h, 64), z7], axis=1)))
    order = np.concatenate([np.concatenate([np.arange(256 * gg, 256 * gg + 256), 512 + np.arange(256 * gg, 256 * gg + 256)])
                            for gg in range(2)])
    wo = w_out[order]
    for kc in range(8):
        out.append(np.ascontiguousarray(wo[kc * 128:(kc + 1) * 128]))
    flat = np.concatenate([p.reshape(-1) for p in out]).astype(np.float32)
    assert flat.size == W_TOTAL, (flat.size, W_TOTAL)
    return flat.reshape(W_TOTAL // 1024, 1024)


_NC_CACHE = {}


def kernel(x, c, positions, w_ada, b_ada, g_pre, w_in, b_fgate, sinks, w_out, g_post):
    x = np.asarray(x, np.float32)
    c = np.asarray(c, np.float32)
    positions = np.asarray(positions, np.int32)
    w_ada = np.ascontiguousarray(np.asarray(w_ada, np.float32)[0])
    b_ada = np.asarray(b_ada, np.float32)[0]
    g_pre = np.asarray(g_pre, np.float32)[0]
    w_in = np.asarray(w_in, np.float32)[0]
    b_fgate = np.asarray(b_fgate, np.float32)[0]
    sinks = np.asarray(sinks, np.float32)[0]
    w_out = np.asarray(w_out, np.float32)[0]
    g_post = np.asarray(g_post, np.float32)[0]

    if "nc" not in _NC_CACHE:
        _NC_CACHE["nc"] = build_nc()
    nc = _NC_CACHE["nc"]

    half = HD // 2
    inv_freq = (10000.0 ** (-np.arange(half, dtype=np.float32) / half)).astype(np.float32)
    invf = (np.tile(inv_freq, 4).astype(np.float64) / (2.0 * math.pi)).reshape(128, 1).astype(np.float32)
    phpi = np.concatenate([np.full(64, 0.25), np.full(32, 0.5), np.zeros(32)]).astype(np.float32).reshape(128, 1)
    wst = [_pieces_for_group(w_in, w_out, g) for g in range(2)]
    w_ada_r = np.ascontiguousarray(w_ada.reshape(8, 128, 24, 128).transpose(2, 1, 0, 3)).reshape(24 * 128, 1024)
    in_maps = []
    for core in range(8):
        b, g = core // 2, core % 2
        in_maps.append({
            "x": np.ascontiguousarray(x[b]),
            "cvec": np.ascontiguousarray(c[b].reshape(8, 128).T),
            "pos": np.ascontiguousarray(np.broadcast_to(positions[b][None, :], (128, S))),
            "w_ada": w_ada_r,
            "b_adaT": np.ascontiguousarray(b_ada.reshape(24, 128).T),
            "b_gate_rep": np.ascontiguousarray(np.broadcast_to(b_ada[2048:3072][None, :], (128, D))),
            "g_preT": np.ascontiguousarray(g_pre.reshape(8, 128).T),
            "g_post_rep": np.ascontiguousarray(np.broadcast_to(g_post[None, :], (128, D))),
            "wstream": wst[g],
            "bfg_rep": np.ascontiguousarray(np.broadcast_to(b_fgate[4 * g:4 * g + 4][None, :], (128, 4))),
            "sink_rep": np.ascontiguousarray(np.broadcast_to(sinks[4 * g:4 * g + 4][None, :], (128, 4))),
            "invf": invf,
            "phpi": phpi,
        })
    res = run_bass_kernel_spmd(nc, in_maps, core_ids=list(range(8)))
    out = np.stack([np.asarray(res.results[2 * b]["y"], np.float32) for b in range(4)], axis=0)
    return out
```

```python
import math
import numpy as np
import concourse.bass as bass
import concourse.mybir as mybir
from concourse.bass_utils import run_bass_kernel_spmd

F32, BF16, I32 = mybir.dt.float32, mybir.dt.bfloat16, mybir.dt.int32
ALU = mybir.AluOpType
AF = mybir.ActivationFunctionType
AX = mybir.AxisListType

S = 8192
D = 1024
T = 512
NT = S // T
HD = 64
EPS = 1e-6
NEG = -30000.0
E_TM, E_FULL, E_QK = 648, 1024, 568
NRING = 5
NY_EST = 321
YCOUNT = []

INPROJ_PIECES = (["tm%d" % i for i in range(4)] + ["zf0", "zf1", "zs0", "zs1"] +
                 ["sq%d" % i for i in range(4)] + ["sk"] +
                 ["fk%d" % i for i in range(4)] + ["fq%d" % i for i in range(4)])
PIECE_E = {}
for _n in INPROJ_PIECES:
    PIECE_E[_n] = E_TM if _n.startswith("tm") else (E_QK if _n[:2] in ("fk", "fq") else E_FULL)
PIECE_OFF = {}
_o = 0
for _n in INPROJ_PIECES:
    PIECE_OFF[_n] = _o
    _o += 128 * PIECE_E[_n]
for _k in range(8):
    PIECE_OFF["wo%d" % _k] = _o
    PIECE_E["wo%d" % _k] = E_FULL
    _o += 128 * E_FULL
W_TOTAL = _o


class Sem:
    def __init__(self, nc, name):
        self.h = nc.alloc_semaphore(name=name)
        self.n = 0


class Prog:
    ENG = ("pe", "act", "dve", "pool", "sp")

    def __init__(self, nc):
        self.nc = nc
        self.ops = {e: [] for e in self.ENG}
        self.prog = {e: Sem(nc, "pg_" + e) for e in ("pe", "act", "dve", "pool")}
        self.waited = {e: {} for e in self.ENG}

    def op(self, eng, fn, waits=(), sig=False, dsem=None, raw_inc=None):
        w = []
        for t in waits:
            if t is None:
                continue
            s, v = t
            if self.waited[eng].get(id(s), 0) >= v:
                continue
            self.waited[eng][id(s)] = v
            w.append((s.h, v))
        tok = None
        inc = None
        if raw_inc is not None:
            inc = (raw_inc.h, None)
            tok = (raw_inc, 1)
        elif dsem is not None:
            dsem.n += 16
            tok = (dsem, dsem.n)
            inc = (dsem.h, 16)
        elif sig:
            s = self.prog[eng]
            s.n += 1
            tok = (s, s.n)
            inc = (s.h, 1)
        self.ops[eng].append((w, fn, inc))
        return tok

    def run(self, eng, e):
        for (w, fn, inc) in self.ops[eng]:
            for (h, v) in w:
                e.wait_ge(h, v)
            if fn is None:
                continue
            ins = fn(e)
            if inc is not None:
                if inc[1] is None:
                    ins.then_inc(inc[0])
                else:
                    ins.then_inc(inc[0], inc[1])


_LAST_PROG = {}


def build_nc():
    nc = bass.Bass("TRN2", target_bir_lowering=False)
    dt_in = lambda name, shape, dt=F32: nc.dram_tensor(name, shape, dt, kind="ExternalInput").ap()
    x_d = dt_in("x", [S, D])
    cvec_d = dt_in("cvec", [128, 8])
    pos_d = dt_in("pos", [128, S], I32)
    wada_d = dt_in("w_ada", [24 * 128, 1024])
    badaT_d = dt_in("b_adaT", [128, 24])
    bgate_d = dt_in("b_gate_rep", [128, D])
    gpreT_d = dt_in("g_preT", [128, 8])
    gpost_d = dt_in("g_post_rep", [128, D])
    wst_d = dt_in("wstream", [W_TOTAL // 1024, 1024])
    bfg_d = dt_in("bfg_rep", [128, 4])
    sink_d = dt_in("sink_rep", [128, 4])
    invf_d = dt_in("invf", [128, 1])
    phpi_d = dt_in("phpi", [128, 1])
    y_d = nc.dram_tensor("y", [S, D], F32, kind="ExternalOutput").ap()

    wbf_t = nc.dram_tensor("wbf", [W_TOTAL // 1024, 1024], BF16)
    tab_t = nc.dram_tensor("tabd", [128, S], BF16)
    gxi_t = [nc.dram_tensor("gxi%d" % t, [512, T], BF16) for t in range(NT)]
    gxo_t = [nc.dram_tensor("gxo%d" % t, [1024, T], BF16) for t in range(NT)]
    wbf = wbf_t.ap()
    tabd = tab_t.ap()

    P = Prog(nc)
    _LAST_PROG['P'] = P
    sb = lambda name, shape, dt: nc.sbuf_tensor(name, shape, dt)
    import contextlib
    with contextlib.ExitStack() as es:
        def SB(name, shape, dt):
            return es.enter_context(nc.sbuf_tensor(name, shape, dt))

        def PS(name, shape, dt):
            return es.enter_context(nc.psum_tensor(name, shape, dt))

        KT = [SB("KT%d" % h, [128, S], BF16) for h in range(4)]
        VC = SB("VC", [128, 64, 4, 64], BF16)
        WR = SB("WR", [128, NRING, 1024], BF16)
        XB = [SB("XB%d" % i, [128, 1024], F32) for i in range(2)]
        SQ = SB("SQ", [128, 1024], F32)
        XR = SB("XR", [128, 1024], F32)
        OT = SB("OT", [128, 1024], F32)
        GG = SB("GG", [128, 1024], F32)
        XN = [SB("XN%d" % i, [128, 1024], BF16) for i in range(2)]
        HT = SB("HT", [128, 8, T], BF16)
        QT2 = [SB("QT%d" % i, [128, 4, T], BF16) for i in range(2)]
        PT = SB("PT", [128, 4, T], BF16)
        SZ2 = [SB("SZ%d" % i, [128, 4, T], BF16) for i in range(2)]
        ZE = SB("ZE", [128, T], F32)
        TAB = [SB("TAB%d" % i, [128, T], BF16) for i in range(2)]
        TT = [SB("TT%d" % i, [128, T], BF16) for i in range(2)]
        QR2 = [SB("QR%d" % i, [128, 4, T], BF16) for i in range(2)]
        KR2 = [SB("KR%d" % i, [128, 640], BF16) for i in range(2)]
        VS2 = [SB("VS%d" % i, [128, 5, 64], BF16) for i in range(2)]
        GT = SB("GT", [128, 4, T], BF16)
        GR = SB("GR", [128, 8, T], BF16)
        T1 = SB("T1", [128, T], F32)
        RB = SB("RB", [128, T], F32)
        ONB = SB("ONB", [128, 64], BF16)
        LP = SB("LP", [128, 4, 4], F32)
        AQ = SB("AQ", [128, 4, 4, 71], BF16)
        AK = SB("AK", [128, 4, 4, 71], BF16)
        SM = SB("SM", [128, 128], F32)
        ss = SM[:, 0:4]
        lnv = SM[:, 4:8]
        rstd = SM[:, 8:12]
        ssy = SM[:, 12:13]
        lny = SM[:, 13:14]
        rsy = SM[:, 14:15]
        scr = SM[:, 16:24]
        sce = SM[:, 24:32]
        sc = SM[:, 32:40]
        modT = SM[:, 40:56]
        Acol = SM[:, 56:64]
        badaT = SM[:, 64:88]
        gpreT = SM[:, 88:96]
        bfrep = SM[:, 96:100]
        sinkr = SM[:, 100:104]
        invf = SM[:, 104:105]
        phpi = SM[:, 105:106]
        sinke = SM[:, 108:112]
        ccar = SM[:, 112:116]
        U16 = SB("U16", [128, 4, 4], F32)
        E16 = SB("E16", [128, 4, 4], F32)
        L16 = SB("L16", [128, 4, 4], F32)
        CP = SB("CP", [128, 4, 4], F32)
        R1 = SB("R1", [128, 4, 4], F32)
        R2 = SB("R2", [128, 4, 4], F32)
        IDB = SB("IDB", [128, 128], BF16)
        IDF = SB("IDF", [128, 128], F32)
        UF = SB("UF", [128, 128], F32)
        EF = SB("EF", [128, 128], F32)
        ONF = SB("ONF", [128, 128], F32)
        MCUR = SB("MCUR", [128, 4, 128], BF16)
        MPRV = SB("MPRV", [128, 4, 128], BF16)
        STK = SB("STK", [128, 64], BF16)
        SPAT = SB("SPAT", [1, T], BF16)
        EPS_AP = SB("EPSAP", [128, 2], F32)

        SB_ = [PS("S%d" % i, [128, T], F32) for i in range(2)]
        OB = [PS("O%d" % i, [128, T], F32) for i in range(2)]
        PA = [PS("PA%d" % i, [128, T], F32) for i in range(3)]
        MI = PS("MI", [128, T], F32)
        TR = MI[:].bitcast(BF16)

        s_small = Sem(nc, "s_small")
        s_xb = [Sem(nc, "s_xb%d" % i) for i in range(2)]
        s_sq = Sem(nc, "s_sq")
        s_stg = [Sem(nc, "s_stg%d" % i) for i in range(2)]
        s_xr = Sem(nc, "s_xr")
        s_cast = Sem(nc, "s_cast")
        s_tabst = [Sem(nc, "s_tabst%d" % i) for i in range(2)]
        s_tab = [Sem(nc, "s_tab%d" % i) for i in range(2)]
        s_ring = [Sem(nc, "s_ring%d" % i) for i in range(NRING)]
        s_gw = Sem(nc, "s_gw")
        s_gr = Sem(nc, "s_gr")
        s_out = Sem(nc, "s_out")
        s_cc = [Sem(nc, "s_cc%d" % t) for t in range(NT)]

        def pool_chain(fns):
            tok = None
            for fn in fns:
                tok = P.op("pool", fn, waits=[tok], sig=True)
            return tok
        sel = lambda out, pattern, op, fill, base, cm: (lambda e: e.affine_select(out=out, in_=out, pattern=pattern,
                                                        compare_op=op, fill=fill, base=base, channel_multiplier=cm))
        t_sm = pool_chain([lambda e: e.memset(SM[:], 0.0)])
        t_idf = pool_chain([lambda e: e.memset(IDF[:], 0.0),
                            sel(IDF[:], [[-1, 128]], ALU.not_equal, 1.0, 0, 1),
                            lambda e: e.tensor_copy(out=IDB[:], in_=IDF[:])])
        pool_chain([lambda e: e.memset(ONF[:], 1.0)])
        pool_chain([lambda e: e.memset(EPS_AP[:, 0:1], EPS), lambda e: e.memset(EPS_AP[:, 1:2], 1.0)])
        c_a = pool_chain([lambda e: e.memset(ONB[:], 1.0)])
        CONST_A = [c_a]
        pool_chain([lambda e: e.memset(UF[:], 1.0), sel(UF[:], [[1, 128]], ALU.is_ge, 0.0, 0, -1)])
        pool_chain([lambda e: e.memset(EF[:], 1.0), sel(EF[:], [[0, 128]], ALU.is_ge, 0.0, -127, 1)])
        pool_chain([lambda e: e.memset(MCUR[:], 0.0), sel(MCUR[:], [[0, 4], [1, 128]], ALU.is_ge, NEG, 0, -1)])
        pool_chain([lambda e: e.memset(MPRV[:], 0.0), sel(MPRV[:], [[0, 4], [-1, 128]], ALU.is_ge, NEG, -1, 1)])
        pool_chain([lambda e: e.memset(STK[:], 0.0), sel(STK[:], [[-1, 64]], ALU.not_equal, 1.0, 0, 1),
                    sel(STK[:], [[-1, 64]], ALU.not_equal, 1.0, -64, 1)])
        pool_chain([lambda e: e.memset(LP[:], 0.0)])
        pool_chain([lambda e: e.memset(AQ[:], 0.0), lambda e: e.memset(AQ[:, :, :, 67:70], 8.0)])
        pool_chain([lambda e: e.memset(AK[:], 0.0), lambda e: e.memset(AK[:, :, :, 64:67], 1.0),
                    lambda e: e.memset(AK[:, :, :, 70:71], 1.0)])
        pool_chain([lambda e: e.memset(KR2[0][:], 0.0)])
        pool_chain([lambda e: e.memset(KR2[1][:], 0.0)])
        pool_chain([lambda e: e.memset(QT2[0][:], 0.0)])
        pool_chain([lambda e: e.memset(QT2[1][:], 0.0)])
        for h_ in range(4):
            c_done = pool_chain([lambda e, h_=h_: e.memset(KT[h_][:], 0.0)])
        CONST = [c_done]

        def small(dst, src):
            return P.op("sp", lambda e: e.dma_start(out=dst, in_=src), waits=[t_sm], dsem=s_small)
        small(scr, cvec_d)
        small(badaT, badaT_d)
        small(gpreT, gpreT_d)
        small(bfrep, bfg_d)
        small(sinkr, sink_d)
        small(invf, invf_d)
        small(OT[:], bgate_d)
        small(XR[:], gpost_d)
        t_small = small(phpi, phpi_d)

        rows = W_TOTAL // 1024
        nch = 4
        rpc = rows // nch
        for i in range(nch):
            t_cast = P.op("pool", lambda e, i=i: e.dma_start(out=wbf[i * rpc:(i + 1) * rpc, :],
                                                         in_=wst_d[i * rpc:(i + 1) * rpc, :]), dsem=s_cast)

        ring = {"i": 0, "free": [None] * NRING}

        def piece_view(name):
            off = PIECE_OFF[name] // 1024
            E = PIECE_E[name]
            return wbf[off:off + 128 * E // 1024, :].flatten().rearrange("(p e) -> p e", p=128)

        def get_piece(name):
            i = ring["i"]
            ring["i"] += 1
            sl = i % NRING
            E = PIECE_E[name]
            src = piece_view(name)
            tok = P.op("sp", lambda e, sl=sl, E=E, src=src: e.dma_start(out=WR[:, sl, 0:E], in_=src),
                       waits=[t_cast, ring["free"][sl]], dsem=s_ring[sl])
            return sl, tok

        def release_piece(sl, tok):
            ring["free"][sl] = tok

        t_a = P.op("act", lambda e: e.activation(out=sce, in_=scr, func=AF.Exp, scale=-1.0), waits=[t_small], sig=True)
        t_d = P.op("dve", lambda e: e.tensor_scalar_add(out=sce, in0=sce, scalar1=1.0), waits=[t_a], sig=True)
        t_d = P.op("dve", lambda e: e.reciprocal(out=sce, in_=sce), waits=[t_d], sig=True)
        t_sc = P.op("dve", lambda e: e.tensor_tensor(out=sc, in0=scr, in1=sce, op=ALU.mult), waits=[t_d], sig=True)
        SQ3 = SQ[:].rearrange("p (k n) -> p k n", k=8)
        for kc in range(8):
            t_screp = P.op("dve", lambda e, kc=kc: e.tensor_scalar(out=SQ3[:, kc, :], in0=ONF[:], scalar1=sc[:, kc:kc + 1],
                           scalar2=None, op0=ALU.mult), waits=[t_sc] + CONST_A, sig=True)
        t_a = P.op("act", lambda e: e.activation(out=sinke, in_=sinkr, func=AF.Exp), waits=[t_small], sig=True)
        for hh in range(4):
            t_spat = P.op("dve", lambda e, hh=hh: e.tensor_scalar(out=SPAT[0:1, hh * 128:(hh + 1) * 128], in0=ONF[0:1, :],
                          scalar1=sinke[0:1, hh:hh + 1], scalar2=None, op0=ALU.mult), waits=[t_a] + CONST_A, sig=True)

        xb_free = [None, None]
        t_pe_last = None
        for j in range(24):
            i = j % 2
            XB3 = XB[i][:].rearrange("p (k n) -> p k n", k=8)
            t_ld = P.op("sp", lambda e, j=j, i=i: e.dma_start(out=XB[i][:], in_=wada_d[j * 128:(j + 1) * 128, :]),
                        waits=[xb_free[i]], dsem=s_stg[i])
            for kc in range(8):
                if j < 16:
                    t_pe = P.op("pe", lambda e, j=j, kc=kc, XB3=XB3: e.matmul(MI[:, j:j + 1], lhsT=XB3[:, kc, :],
                                rhs=sc[:, kc:kc + 1], start=(kc == 0), stop=(kc == 7)),
                                waits=[t_ld, t_sc], sig=(kc == 7))
                else:
                    jj = j - 16
                    bank = PA[jj // 4]
                    c0 = (jj % 4) * 128
                    t_pe = P.op("pe", lambda e, kc=kc, XB3=XB3, bank=bank, c0=c0: e.matmul(bank[:, c0:c0 + 128],
                                lhsT=SQ3[:, kc, :], rhs=XB3[:, kc, :], start=(kc == 0), stop=(kc == 7)),
                                waits=[t_ld, t_screp], sig=(kc == 7))
            xb_free[i] = t_pe
            t_pe_last = t_pe
        ZEi = ZE[:].bitcast(I32)
        t_prev_copy = None
        t_sin = None
        t_store = [None, None]
        for ch in range(16):
            xi = ch % 2
            t_ld = P.op("act", lambda e, ch=ch: e.dma_start(out=ZEi, in_=pos_d[:, ch * 512:(ch + 1) * 512]),
                        waits=[t_prev_copy], dsem=s_sq)
            t_d = P.op("dve", lambda e: e.tensor_copy(out=T1[:], in_=ZEi), waits=[t_ld, t_sin], sig=True)
            t_d = P.op("dve", lambda e: e.tensor_scalar(out=T1[:], in0=T1[:], scalar1=invf, scalar2=phpi,
                       op0=ALU.mult, op1=ALU.add), waits=[t_d, t_small], sig=True)
            t_d = P.op("dve", lambda e: e.tensor_copy(out=ZEi, in_=T1[:]), waits=[t_d], sig=True)
            t_d = P.op("dve", lambda e: e.tensor_copy(out=RB[:], in_=ZEi), waits=[t_d], sig=True)
            t_prev_copy = t_d
            t_d = P.op("dve", lambda e: e.tensor_tensor(out=T1[:], in0=T1[:], in1=RB[:], op=ALU.subtract), waits=[t_d], sig=True)
            t_sin = P.op("act", lambda e, xi=xi: e.activation(out=XN[xi][:, 0:512], in_=T1[:], func=AF.Sin, scale=2.0 * math.pi),
                         waits=[t_d, t_store[xi]], sig=True)
            t_store[xi] = P.op("pool", lambda e, ch=ch, xi=xi: e.dma_start(out=tabd[:, ch * 512:(ch + 1) * 512], in_=XN[xi][:, 0:512]),
                               waits=[t_sin], dsem=s_tabst[xi])
        t_tab_done = t_store[1]
        t_tab_done0 = t_store[0]
        t_rope_dve = t_d

        t_d = P.op("dve", lambda e: e.tensor_tensor(out=modT, in0=MI[:, 0:16], in1=badaT[:, 0:16], op=ALU.add),
                   waits=[t_pe_last, t_small], sig=True)
        t_d = P.op("dve", lambda e: e.tensor_scalar_add(out=Acol, in0=modT[:, 8:16], scalar1=1.0), waits=[t_d], sig=True)
        t_mod = P.op("dve", lambda e: e.tensor_tensor(out=Acol, in0=Acol, in1=gpreT, op=ALU.mult), waits=[t_d], sig=True)
        SHcol = modT[:, 0:8]
        for hf in range(2):
            t_d = P.op("dve", lambda e, hf=hf: e.tensor_tensor(out=GG[:, hf * 512:(hf + 1) * 512], in0=PA[hf][:],
                       in1=OT[:, hf * 512:(hf + 1) * 512], op=ALU.add), waits=[t_pe_last, t_small], sig=True)
        t_gg = P.op("dve", lambda e: e.tensor_tensor(out=GG[:], in0=GG[:], in1=XR[:], op=ALU.mult), waits=[t_d], sig=True)
        mi_free = t_mod
        pa_free = [t_gg, t_gg, None]

        st = dict(
            n=0, qk=[], ex=[], pv=[],
            om=0, o_free=[None, None],
            pa_i=0, pa_free=pa_free,
            mi_free=mi_free,
            tr_free=None,
            ht_free=None,
            xn_free=[t_tab_done0, t_tab_done],
            sq_free=t_pe_last, xr_free=t_gg, ot_free=t_gg,
            aux_free=None,
            x_scaled=[xb_free[0], xb_free[1]],
            gt_free=None, gr_free=None, ze_free=t_rope_dve, rb_free=t_rope_dve,
            pending=[],
        )
        TS = {}
        gtw = []

        def acquire_pa():
            i = st["pa_i"] % 3
            st["pa_i"] += 1
            return i, PA[i], st["pa_free"][i]

        def phase_a(t):
            toks = []
            TS.setdefault(t, {})["ht"] = toks
            for nb in range(4):
                gb = 4 * t + nb
                i = gb % 2
                r0 = gb * 128
                t_ld = P.op("pool", lambda e, i=i, r0=r0: e.dma_start(out=XB[i][:], in_=x_d[r0:r0 + 128, :]),
                            waits=[st["x_scaled"][i]], dsem=s_xb[i])
                t_p = P.op("pool", lambda e, i=i: e.tensor_tensor(out=SQ[:], in0=XB[i][:], in1=XB[i][:], op=ALU.mult),
                           waits=[t_ld, st["sq_free"]], sig=True)
                st["sq_free"] = None
                yield
                yield
                yield
                t_d = P.op("dve", lambda e, nb=nb: e.tensor_reduce(out=ss[:, nb:nb + 1], in_=SQ[:], axis=AX.X, op=ALU.add),
                           waits=[t_p], sig=True)
                st["sq_free"] = t_d
                for _ in range(8):
                    yield
                t_a = P.op("act", lambda e, nb=nb: e.activation(out=lnv[:, nb:nb + 1], in_=ss[:, nb:nb + 1], func=AF.Ln,
                           scale=1.0 / D, bias=EPS_AP[:, 0:1]), waits=[t_d], sig=True)
                t_a = P.op("act", lambda e, nb=nb: e.activation(out=rstd[:, nb:nb + 1], in_=lnv[:, nb:nb + 1], func=AF.Exp,
                           scale=-0.5), waits=[t_a], sig=True)
                yield
                yield
                t_xn = P.op("dve", lambda e, i=i, nb=nb: e.tensor_scalar(out=XN[i][:], in0=XB[i][:], scalar1=rstd[:, nb:nb + 1],
                            scalar2=None, op0=ALU.mult), waits=[t_a, t_ld, st["xn_free"][i]], sig=True)
                st["x_scaled"][i] = t_xn
                yield
                for fc in range(8):
                    t_pe = P.op("pe", lambda e, i=i, fc=fc: e.transpose(TR[:, fc * 128:(fc + 1) * 128],
                                XN[i][:, fc * 128:(fc + 1) * 128], IDB[:]),
                                waits=[t_xn, st["mi_free"]] + CONST, sig=(fc == 7))
                    if fc % 2 == 1 and fc < 7:
                        yield
                st["xn_free"][i] = t_pe
                yield
                yield
                for fc in range(8):
                    t_e = P.op("dve", lambda e, fc=fc, nb=nb: e.tensor_scalar(out=HT[:, fc, nb * 128:(nb + 1) * 128],
                               in0=TR[:, fc * 128:(fc + 1) * 128], scalar1=Acol[:, fc:fc + 1], scalar2=SHcol[:, fc:fc + 1],
                               op0=ALU.mult, op1=ALU.add), waits=[t_pe, st["ht_free"], t_mod], sig=(fc == 7))
                st["mi_free"] = t_e
                toks.append(t_e)
                yield

        def phase_b(t):
            ts = TS.setdefault(t, {})
            ht_ready = ts["ht"]
            par = t % 2
            prev = TS.get(t - 1, {})
            prev2 = TS.get(t - 2, {})
            QTp, SZp, QRp, KRp, VSp = QT2[par], SZ2[par], QR2[par], KR2[par], VS2[par]
            t_tab = P.op("pool", lambda e: e.dma_start(out=TAB[par][:], in_=tabd[:, t * T:(t + 1) * T]),
                         waits=[t_tab_done, t_tab_done0, prev2.get("tab_free")], dsem=s_tab[par])
            carry_tok = []
            if t > 0:
                KRq, VSq = KR2[1 - par], VS2[1 - par]
                tk = P.op("dve", lambda e: e.tensor_copy(out=KRp[0:64, 0:128], in_=KRq[0:64, 512:640]),
                          waits=[prev.get("kr_tok"), prev2.get("swa_qk_last")], sig=True)
                tv = P.op("dve", lambda e: e.tensor_copy(out=VSp[:, 0, :], in_=VSq[:, 4, :]),
                          waits=[prev.get("vs_tok"), prev2.get("swa_pv_last")], sig=True)
                carry_tok = [tk, tv]
            swa_free = [prev2.get("swa_qk_last"), prev2.get("swa_pv_last")]
            tm = [get_piece("tm%d" % i) for i in range(4)]
            t_u_last = None
            vc_tok = []
            for nb in range(4):
                pi, bank, bfree = acquire_pa()
                for kc in range(8):
                    sl, ltok = tm[kc // 2]
                    c0 = (kc % 2) * 324
                    t_pe = P.op("pe", lambda e, bank=bank, kc=kc, nb=nb, sl=sl, c0=c0: e.matmul(bank[:, 0:324],
                                lhsT=HT[:, kc, nb * 128:(nb + 1) * 128], rhs=WR[:, sl, c0:c0 + 324],
                                start=(kc == 0), stop=(kc == 7)),
                                waits=[ltok, bfree, ht_ready[nb]], sig=(kc == 7))
                    if kc % 2 == 1 and kc < 7:
                        yield
                gb = 4 * t + nb
                yield
                t1 = P.op("dve", lambda e, bank=bank, gb=gb: e.tensor_copy(out=VC[:, gb, :, :],
                          in_=bank[:, 0:256].rearrange("p (h d) -> p h d", h=4)), waits=[t_pe], sig=True)
                t2 = P.op("dve", lambda e, bank=bank, nb=nb: e.tensor_tensor(out=U16[:, nb, :], in0=bank[:, 256:260],
                          in1=bfrep, op=ALU.add), waits=[t_pe, t_small, st.get("u_free")], sig=True)
                t3 = P.op("dve", lambda e, bank=bank, nb=nb: e.tensor_copy(out=VSp[:, 1 + nb, :], in_=bank[:, 260:324]),
                          waits=[t_pe] + carry_tok + swa_free, sig=True)
                st["pa_free"][pi] = t3
                t_u_last = t2
                vc_tok.append(t1)
                ts["vs_tok"] = t3
                yield
            for (sl, ltok) in tm:
                release_piece(sl, t_pe)
            ts["vc_tok"] = vc_tok
            yield
            t_a = P.op("act", lambda e: e.activation(out=E16[:], in_=U16[:], func=AF.Exp, scale=-1.0), waits=[t_u_last], sig=True)
            t_l = P.op("act", lambda e: e.activation(out=L16[:], in_=E16[:], func=AF.Ln, bias=EPS_AP[:, 1:2]),
                       waits=[t_a, st.get("l_free")], sig=True)
            st["u_free"] = t_a
            yield
            t_lp = P.op("dve", lambda e: e.tensor_copy(out=LP[:, 1, :], in_=L16[:, 0, :]), waits=[t_l, st.get("l_free")], sig=True)
            t_lp = P.op("dve", lambda e: e.tensor_tensor(out=LP[:, 2, :], in0=LP[:, 1, :], in1=L16[:, 1, :], op=ALU.add), waits=[t_lp], sig=True)
            t_lp = P.op("dve", lambda e: e.tensor_tensor(out=LP[:, 3, :], in0=LP[:, 2, :], in1=L16[:, 2, :], op=ALU.add), waits=[t_lp], sig=True)
            MI16 = MI[:, 0:16]
            P.op("pe", lambda e: e.matmul(MI16, lhsT=UF[:], rhs=L16[:].rearrange("p a b -> p (a b)"), start=True, stop=False),
                 waits=[t_l, st["mi_free"], st.get("ccar_tok")] + CONST)
            P.op("pe", lambda e: e.matmul(MI16, lhsT=ONF[:], rhs=LP[:].rearrange("p a b -> p (a b)"), start=False, stop=False),
                 waits=[t_lp])
            for nb in range(4):
                t_pe = P.op("pe", lambda e, nb=nb: e.matmul(MI[:, nb * 4:(nb + 1) * 4], lhsT=EF[:], rhs=ccar, start=False, stop=(nb == 3)),
                            sig=(nb == 3))
            yield
            st["l_free"] = t_pe
            yield
            t_c = P.op("dve", lambda e: e.tensor_copy(out=CP[:], in_=MI[:, 0:16].rearrange("p (a b) -> p a b", a=4)),
                       waits=[t_pe], sig=True)
            st["mi_free"] = t_c
            st["ccar_tok"] = P.op("dve", lambda e: e.tensor_copy(out=ccar, in_=CP[:, 3, :]), waits=[t_c], sig=True)
            afree = st["aux_free"]
            t_d = P.op("dve", lambda e: e.tensor_copy(out=AK[:, :, :, 67], in_=CP[:]), waits=[t_c, afree], sig=True)
            t_d = P.op("dve", lambda e: e.tensor_tensor(out=R1[:], in0=CP[:], in1=AK[:, :, :, 67], op=ALU.subtract), waits=[t_d], sig=True)
            t_d = P.op("dve", lambda e: e.tensor_copy(out=AK[:, :, :, 68], in_=R1[:]), waits=[t_d], sig=True)
            t_d = P.op("dve", lambda e: e.tensor_tensor(out=R2[:], in0=R1[:], in1=AK[:, :, :, 68], op=ALU.subtract), waits=[t_d], sig=True)
            t_d = P.op("dve", lambda e: e.tensor_copy(out=AK[:, :, :, 69], in_=R2[:]), waits=[t_d], sig=True)
            t_aux = P.op("dve", lambda e: e.tensor_scalar(out=AQ[:, :, :, 64:67], in0=AK[:, :, :, 67:70], scalar1=-8.0,
                         scalar2=None, op0=ALU.mult), waits=[t_d], sig=True)
            yield

            def fm_group(pname, M, res, aux=None):
                sl, ltok = get_piece(pname)
                pi, bank, bfree = acquire_pa()
                E = PIECE_E[pname]
                Wv = WR[:, sl, 0:E].rearrange("p (k m) -> p k m", k=8)
                for kc in range(8):
                    t_pe = P.op("pe", lambda e, bank=bank, kc=kc, Wv=Wv, M=M: e.matmul(bank[0:M, :], lhsT=Wv[:, kc, 0:M],
                                rhs=HT[:, kc, :], start=(kc == 0), stop=(kc == 7 and aux is None)),
                                waits=[ltok, bfree] + ht_ready, sig=(kc == 7))
                    if kc % 2 == 1 and kc < 7:
                        yield
                release_piece(sl, t_pe)
                if aux is not None:
                    A_, h = aux
                    for nb in range(4):
                        t_pe = P.op("pe", lambda e, bank=bank, nb=nb, A_=A_, h=h: e.matmul(bank[0:71, nb * 128:(nb + 1) * 128],
                                    lhsT=A_[:, nb, h, :], rhs=IDB[:], start=False, stop=(nb == 3)),
                                    waits=[t_aux], sig=(nb == 3))
                res["pi"], res["bank"], res["t_pe"] = pi, bank, t_pe

            sz_free = prev2.get("g_last")
            ts["sz_ready"] = [None] * 4
            for ci, pname in enumerate(["zf0", "zf1", "zs0", "zs1"]):
                r = {}
                yield from fm_group(pname, 128, r)
                pi, bank, t_pe = r["pi"], r["bank"], r["t_pe"]
                yield
                yield
                t_a = P.op("act", lambda e, bank=bank: e.activation(out=ZE[:], in_=bank[:], func=AF.Exp, scale=-1.0),
                           waits=[t_pe, st.get("ze_free")], sig=True)
                t_a = P.op("act", lambda e: e.activation(out=ZE[:], in_=ZE[:], func=AF.Ln, bias=EPS_AP[:, 1:2]), waits=[t_a], sig=True)
                t_a = P.op("act", lambda e: e.activation(out=ZE[:], in_=ZE[:], func=AF.Exp, scale=-1.0), waits=[t_a], sig=True)
                yield
                yield
                t_d = P.op("dve", lambda e, bank=bank, ci=ci: e.tensor_tensor(out=SZp[:, ci, :], in0=bank[:], in1=ZE[:], op=ALU.mult),
                           waits=[t_a, sz_free], sig=True)
                st["ze_free"] = t_d
                st["pa_free"][pi] = t_d
                ts["sz_ready"][ci] = t_d
                yield
            for idx, pname in enumerate(["sq0", "sq1", "sq2", "sq3", "sk"]):
                r = {}
                yield from fm_group(pname, 128, r)
                pi, bank, t_pe = r["pi"], r["bank"], r["t_pe"]
                j = idx % 2
                yield
                t_d = P.op("dve", lambda e, bank=bank, j=j: e.tensor_tensor(out=TT[j][:], in0=bank[:], in1=TAB[par][:], op=ALU.mult),
                           waits=[t_pe, t_tab, st.get("tt_free%d" % j)], sig=True)
                st["pa_free"][pi] = t_d
                t_p2 = P.op("pe", lambda e, j=j: e.matmul(MI[0:64, :], lhsT=STK[:], rhs=TT[j][:], start=True, stop=True),
                            waits=[t_d, st["mi_free"]] + CONST, sig=True)
                st["tt_free%d" % j] = t_p2
                yield
                if idx < 4:
                    t_e = P.op("dve", lambda e, idx=idx: e.tensor_copy(out=QRp[0:64, idx, :], in_=MI[0:64, :]),
                               waits=[t_p2] + swa_free, sig=True)
                    ts["qr_tok"] = t_e
                else:
                    t_e = P.op("dve", lambda e: e.tensor_copy(out=KRp[0:64, 128:640], in_=MI[0:64, :]),
                               waits=[t_p2] + carry_tok + swa_free, sig=True)
                    ts["kr_tok"] = t_e
                st["mi_free"] = t_e
                ts["tab_free"] = t_d
                yield
            kt_tok = []
            for h in range(4):
                r = {}
                yield from fm_group("fk%d" % h, 71, r, aux=(AK, h))
                pi, bank, t_pe = r["pi"], r["bank"], r["t_pe"]
                yield
                t_e = P.op("dve", lambda e, bank=bank, h=h: e.tensor_copy(out=KT[h][0:71, t * T:(t + 1) * T], in_=bank[0:71, :]),
                           waits=[t_pe], sig=True)
                st["pa_free"][pi] = t_e
                kt_tok.append(t_e)
                yield
            ts["kt_tok"] = kt_tok
            qt_tok = []
            for h in range(4):
                r = {}
                yield from fm_group("fq%d" % h, 71, r, aux=(AQ, h))
                pi, bank, t_pe = r["pi"], r["bank"], r["t_pe"]
                yield
                t_e = P.op("dve", lambda e, bank=bank, h=h: e.tensor_copy(out=QTp[0:71, h, :], in_=bank[0:71, :]),
                           waits=[t_pe, prev2.get("qt_last")], sig=True)
                st["pa_free"][pi] = t_e
                qt_tok.append(t_e)
                yield
            ts["qt_tok"] = qt_tok
            st["aux_free"] = t_pe
            st["ht_free"] = t_pe

        def flush_pending(force=False):
            keep = []
            items = st["pending"]
            st["pending"] = []
            for item in items:
                item[0] -= 1
                if item[0] <= 0 or force:
                    item[1]()
                else:
                    keep.append(item)
            st["pending"] = keep + st["pending"]

        def attention(t, bg, bg_steps):
            ts = TS[t]
            par = t % 2
            QTp, SZp, QRp, KRp, VSp = QT2[par], SZ2[par], QR2[par], KR2[par], VS2[par]
            vc_tok, kt_tok, qt_tok = ts["vc_tok"], ts["kt_tok"], ts["qt_tok"]
            tiles = []
            for h in range(4):
                nkb = 4 * t + 4
                for kb in range(nkb):
                    j = kb - 4 * t
                    tiles.append(dict(kind="fox", h=h, kb=kb, c0=(128 * j if j >= 0 else 0), diag=(j >= 0),
                                      first=(kb == 0), last=(kb == nkb - 1)))
            for nb in range(4):
                seq = []
                if not (t == 0 and nb == 0):
                    seq.append("prev")
                seq.append("cur")
                for k, which in enumerate(seq):
                    tiles.append(dict(kind="swa", nb=nb, which=which, first=(k == 0), last=(k == len(seq) - 1)))

            def emit_qk(tl):
                n = st["n"]
                tl["n"] = n
                st["n"] += 1
                Sb = SB_[n % 2]
                sfree = st["ex"][n - 2] if n >= 2 else None
                if tl["kind"] == "fox":
                    h, kb, c0 = tl["h"], tl["kb"], tl["c0"]
                    w = [sfree, qt_tok[h], kt_tok[h]]
                    tq = P.op("pe", lambda e: e.matmul(Sb[:, c0:T], lhsT=KT[h][:, kb * 128:(kb + 1) * 128],
                              rhs=QTp[:, h, c0:T], start=True, stop=(not tl["diag"])), waits=w + CONST, sig=(not tl["diag"]))
                    if tl["diag"]:
                        tq = P.op("pe", lambda e: e.matmul(Sb[:, c0:c0 + 128], lhsT=IDB[:], rhs=MCUR[:, 0, :],
                                  start=False, stop=True), waits=CONST, sig=True)
                    ts["qt_last"] = tq
                else:
                    nb, which = tl["nb"], tl["which"]
                    off = nb * 128 if which == "prev" else (nb + 1) * 128
                    Sv = Sb[:].rearrange("p (h q) -> p h q", h=4)
                    M_ = MPRV if which == "prev" else MCUR
                    w = [sfree, ts["qr_tok"], ts["kr_tok"]]
                    P.op("pe", lambda e: e.matmul(Sv, lhsT=KRp[0:64, off:off + 128], rhs=QRp[0:64, :, nb * 128:(nb + 1) * 128],
                         start=True, stop=False), waits=w)
                    tq = P.op("pe", lambda e: e.matmul(Sv, lhsT=IDB[:], rhs=M_[:], start=False, stop=True), waits=CONST, sig=True)
                    ts["swa_qk_last"] = tq
                st["qk"].append(tq)
                c0 = tl.get("c0", 0)
                pfree = st["pv"][n - 4] if n >= 4 else None
                te = P.op("act", lambda e: e.activation(out=PT[:, n % 4, c0:T], in_=Sb[:, c0:T], func=AF.Exp, scale=0.125),
                          waits=[tq, pfree], sig=True)
                st["ex"].append(te)

            def emit_pv(tl):
                n = tl["n"]
                c0 = tl.get("c0", 0)
                if tl["first"]:
                    st["cur_ob"] = st["om"] % 2
                    st["om"] += 1
                    fin_prev = st.setdefault("o_fin", [None, None])[st["cur_ob"]]
                    if fin_prev is not None:
                        fin_prev()
                ob = st["cur_ob"]
                Ob = OB[ob]
                w = [st["ex"][n]]
                if tl["first"]:
                    w.append(st["o_free"][ob])
                if tl["kind"] == "fox":
                    h, kb = tl["h"], tl["kb"]
                    w.append(vc_tok[min(3, max(0, kb - 4 * t))])
                    P.op("pe", lambda e: e.matmul(Ob[0:64, c0:T], lhsT=VC[:, kb, h, :], rhs=PT[:, n % 4, c0:T],
                         start=tl["first"], stop=tl["last"]), waits=w)
                    tp = P.op("pe", lambda e: e.matmul(Ob[64:128, c0:T], lhsT=ONB[:], rhs=PT[:, n % 4, c0:T],
                              start=tl["first"], stop=tl["last"]), waits=CONST, sig=True)
                    st["pv"].append(tp)
                    if tl["last"]:
                        head_epilogue(ob, tp, fox_h=h)
                else:
                    nb, which = tl["nb"], tl["which"]
                    slot = nb if which == "prev" else nb + 1
                    w.append(ts["vs_tok"])
                    P.op("pe", lambda e: e.matmul(Ob[0:64, :], lhsT=VSp[:, slot, :], rhs=PT[:, n % 4, :],
                         start=tl["first"], stop=tl["last"]), waits=w)
                    tp = P.op("pe", lambda e: e.matmul(Ob[64:128, :], lhsT=ONB[:], rhs=PT[:, n % 4, :],
                              start=tl["first"], stop=False), waits=CONST, sig=True)
                    st["pv"].append(tp)
                    if tl["last"]:
                        tp2 = P.op("pe", lambda e: e.matmul(Ob[64:128, :], lhsT=ONB[0:1, :], rhs=SPAT[0:1, :], start=False, stop=True),
                                   waits=[t_spat] + CONST, sig=True)
                        ts["swa_pv_last"] = tp2
                        head_epilogue(ob, tp2, swa_nb=nb)

            def head_epilogue(ob, tp, fox_h=None, swa_nb=None):
                Ob = OB[ob]
                H = {"s1": False, "s2": None}
                rq_free = st.setdefault("rq_free", {})
                rq_fin = st.setdefault("rq_fin", {})
                if fox_h is not None:
                    r0f = (fox_h % 2) * 64
                    quads = [(fox_h % 2, 0), (fox_h % 2, 1)]
                    parts = [(r0f, Ob[64:128, :], RB[r0f:r0f + 64, :])]
                else:
                    cb = (swa_nb % 2) * 256
                    quads = [(0, swa_nb % 2), (1, swa_nb % 2)]
                    Ov = Ob[64:128, :].rearrange("p (a b q) -> p a b q", a=2, b=2)
                    Rv = [RB[r * 64:(r + 1) * 64, cb:cb + 256].rearrange("p (a q) -> p a q", a=2) for r in range(2)]
                    Tv = [T1[r * 64:(r + 1) * 64, cb:cb + 256].rearrange("p (a q) -> p a q", a=2) for r in range(2)]
                    parts = [(0, Ov[:, :, 0, :], Rv[0]), (64, Ov[:, :, 1, :], Rv[1])]

                prev_fins = [rq_fin.get(q) for q in quads]

                def stage1():
                    if H["s1"]:
                        return
                    H["s1"] = True
                    for f in prev_fins:
                        if f is not None:
                            f()
                    t_rs = []
                    for (r0, src, dst) in parts:
                        t_a = P.op("act", lambda e, src=src, dst=dst: e.activation(out=dst, in_=src, func=AF.Ln),
                                   waits=[tp] + [rq_free.get(q, st.get("rb_free")) for q in quads], sig=True)
                        t_r = P.op("act", lambda e, dst=dst: e.activation(out=dst, in_=dst, func=AF.Exp, scale=-1.0),
                                   waits=[t_a], sig=True)
                        t_rs.append(t_r)

                    def stage2_body():
                        if fox_h is not None:
                            h = fox_h
                            r0 = (h % 2) * 64
                            ci = h // 2
                            t_1 = P.op("dve", lambda e: e.tensor_tensor(out=T1[r0:r0 + 64, :], in0=Ob[0:64, :], in1=RB[r0:r0 + 64, :], op=ALU.mult),
                                       waits=[t_rs[0]], sig=True)
                            t_g = P.op("dve", lambda e: e.tensor_tensor(out=GT[r0:r0 + 64, ci, :], in0=T1[r0:r0 + 64, :], in1=SZp[r0:r0 + 64, ci, :], op=ALU.mult),
                                       waits=[t_1, st["gt_free"], ts["sz_ready"][ci]], sig=True)
                            t_o = t_1
                        else:
                            nb = swa_nb
                            O4 = Ob[0:64, :].rearrange("p (a b q) -> p a b q", a=2, b=2)
                            t_g = None
                            for par_ in range(2):
                                r0 = par_ * 64
                                t_1 = P.op("dve", lambda e, par_=par_: e.tensor_tensor(out=Tv[par_], in0=O4[:, :, par_, :],
                                           in1=Rv[par_], op=ALU.mult), waits=[t_rs[par_], t_g], sig=True)
                                t_g = P.op("dve", lambda e, r0=r0, par_=par_: e.tensor_tensor(out=GT[r0:r0 + 64, 2:4, nb * 128:(nb + 1) * 128],
                                           in0=Tv[par_], in1=SZp[r0:r0 + 64, 2:4, nb * 128:(nb + 1) * 128], op=ALU.mult),
                                           waits=[t_1, st["gt_free"], ts["sz_ready"][2], ts["sz_ready"][3]], sig=True)
                            t_o = t_1
                        for q in quads:
                            rq_free[q] = t_g
                        st["o_free"][ob] = t_o
                        ts["g_last"] = t_g
                        gtw.append(t_g)
                    done = {"d": False}

                    def stage2():
                        if not done["d"]:
                            done["d"] = True
                            stage2_body()
                    H["s2"] = stage2
                    st["pending"].append([6 if fox_h is not None else 4, stage2])

                def fin():
                    stage1()
                    H["s2"]()
                for q in quads:
                    rq_fin[q] = fin
                st.setdefault("o_fin", [None, None])[ob] = fin
                st["pending"].append([3, stage1])

            acc = 0.0
            for i, tl in enumerate(tiles):
                emit_qk(tl)
                acc += bg_steps
                while acc >= 1.0:
                    if next(bg, "END") != "END":
                        st["bg_used"] = st.get("bg_used", 0) + 1
                    acc -= 1.0
                if i >= 1:
                    emit_pv(tiles[i - 1])
                flush_pending()
            emit_pv(tiles[-1])
            for _ in range(4):
                flush_pending(force=True)
            nleft = 0
            for _ in bg:
                nleft += 1
            YCOUNT.append((t, st.get("bg_used", 0), nleft))
            st["bg_used"] = 0

        def exchange(t):
            toks = list(gtw)
            del gtw[:]
            gi = gxi_t[t].ap()
            go = gxo_t[t].ap()
            t_w = P.op("pool", lambda e: e.dma_start(out=gi.rearrange("(c p) n -> p c n", p=128), in_=GT[:]),
                       waits=toks, dsem=s_gw)
            st["gt_free"] = t_w
            t_c = P.op("pool", lambda e: e.collective_compute("AllGather", ALU.bypass,
                       replica_groups=[[0, 1], [2, 3], [4, 5], [6, 7]], ins=[gi.opt()], outs=[go.opt()]),
                       waits=[t_w], raw_inc=s_cc[t])
            return t_c

        def epilogue(t, t_c, lead=0):
            for _ in range(lead):
                yield
            go = gxo_t[t].ap()
            gov = go.rearrange("(c p) n -> p c n", p=128)
            t_gr = P.op("pool", lambda e: e.dma_start(out=GR[:], in_=gov), waits=[t_c, st.get("gr_free")], dsem=s_gr)
            for nb in range(4):
                r0 = t * T + nb * 128
                p0, b0, f0 = acquire_pa()
                p1, b1, f1 = acquire_pa()
                banks = [b0, b1]
                for kc in range(8):
                    sl, ltok = get_piece("wo%d" % kc)
                    for hf in range(2):
                        t_pe = P.op("pe", lambda e, kc=kc, hf=hf, sl=sl, nb=nb, banks=banks: e.matmul(banks[hf][:],
                                    lhsT=GR[:, kc, nb * 128:(nb + 1) * 128], rhs=WR[:, sl, hf * 512:(hf + 1) * 512],
                                    start=(kc == 0), stop=(kc == 7)), waits=[ltok, t_gr, f0, f1], sig=(hf == 1))
                    release_piece(sl, t_pe)
                    yield
                st["gr_free"] = t_pe
                yield
                for hf in range(2):
                    t_y = P.op("dve", lambda e, hf=hf, banks=banks: e.tensor_copy(out=OT[:, hf * 512:(hf + 1) * 512], in_=banks[hf][:]),
                               waits=[t_pe, st["ot_free"]], sig=True)
                st["pa_free"][p0] = t_y
                st["pa_free"][p1] = t_y
                yield
                t_p = P.op("pool", lambda e: e.tensor_tensor(out=XR[:], in0=OT[:], in1=OT[:], op=ALU.mult),
                           waits=[t_y, st["xr_free"]], sig=True)
                yield
                yield
                yield
                t_d = P.op("dve", lambda e: e.tensor_reduce(out=ssy, in_=XR[:], axis=AX.X, op=ALU.add), waits=[t_p], sig=True)
                t_x = P.op("pool", lambda e, r0=r0: e.dma_start(out=XR[:], in_=x_d[r0:r0 + 128, :]),
                           waits=[t_d], dsem=s_xr)
                for _ in range(8):
                    yield
                t_a = P.op("act", lambda e: e.activation(out=lny, in_=ssy, func=AF.Ln, scale=1.0 / D, bias=EPS_AP[:, 0:1]),
                           waits=[t_d], sig=True)
                t_a = P.op("act", lambda e: e.activation(out=rsy, in_=lny, func=AF.Exp, scale=-0.5), waits=[t_a], sig=True)
                yield
                t_d = P.op("dve", lambda e: e.scalar_tensor_tensor(out=OT[:], in0=OT[:], scalar=rsy, in1=GG[:],
                           op0=ALU.mult, op1=ALU.mult), waits=[t_a, t_gg], sig=True)
                yield
                t_p = P.op("pool", lambda e: e.tensor_tensor(out=XR[:], in0=OT[:], in1=XR[:], op=ALU.add),
                           waits=[t_d, t_x], sig=True)
                st["ot_free"] = t_p
                t_o = P.op("pool", lambda e, r0=r0: e.dma_start(out=y_d[r0:r0 + 128, :], in_=XR[:]), waits=[t_p], dsem=s_out)
                st["xr_free"] = t_o
                st["t_out"] = t_o
                yield

        import itertools
        for _ in phase_a(0):
            pass
        for _ in phase_b(0):
            pass
        cc_tok = {}
        for t in range(NT):
            par_gens = []
            if t + 1 < NT:
                par_gens.append(phase_a(t + 1))
            if t >= 1:
                par_gens.append(epilogue(t - 1, cc_tok[t - 1], lead=40))
            def rr(gs):
                gs = list(gs)
                while gs:
                    for g_ in list(gs):
                        try:
                            next(g_)
                            yield
                        except StopIteration:
                            gs.remove(g_)
            bg = itertools.chain(rr(par_gens), phase_b(t + 1) if t + 1 < NT else iter(()))
            ntiles = 16 * (t + 1) + 8
            bg_steps = float(NY_EST) / ntiles
            attention(t, bg, bg_steps)
            cc_tok[t] = exchange(t)
        for _ in epilogue(NT - 1, cc_tok[NT - 1]):
            pass
        P.op("pool", None, waits=[st["t_out"]])

        with nc.Block() as block:
            @block.tensor
            def _(e):
                P.run("pe", e)

            @block.scalar
            def _(e):
                P.run("act", e)

            @block.vector
            def _(e):
                P.run("dve", e)

            @block.gpsimd
            def _(e):
                P.run("pool", e)

            @block.sync
            def _(e):
                P.run("sp", e)
    return nc


IN_OFF = {"fq": 0, "fk": 512, "fv": 1024, "f": 1536, "fz": 1544, "sq": 2056, "sk": 2568, "sv": 2696, "sz": 2824}


def _pieces_for_group(w_in, w_out, g):
    def cols(base, lo, n):
        return w_in[:, base + lo: base + lo + n]
    out = []
    tmw = np.concatenate([cols(IN_OFF["fv"], 256 * g, 256), cols(IN_OFF["f"], 4 * g, 4), cols(IN_OFF["sv"], 64 * g, 64)], axis=1)
    tm4 = tmw.reshape(8, 128, 324)
    for i in range(4):
        out.append(np.ascontiguousarray(tm4[2 * i:2 * i + 2].transpose(1, 0, 2)).reshape(128, 648))

    def fm(wc):
        M = wc.shape[1]
        return np.ascontiguousarray(wc.reshape(8, 128, M).transpose(1, 0, 2)).reshape(128, 8 * M)
    for ci in range(2):
        out.append(fm(cols(IN_OFF["fz"], 256 * g + 128 * ci, 128)))
    for ci in range(2):
        out.append(fm(cols(IN_OFF["sz"], 256 * g + 128 * ci, 128)))
    perm = np.concatenate([np.arange(32, 64), np.arange(0, 32)])
    for h in range(4):
        q = cols(IN_OFF["sq"], 256 * g + 64 * h, 64)
        out.append(fm(np.concatenate([q, q[:, perm]], axis=1)))
    k = cols(IN_OFF["sk"], 64 * g, 64)
    out.append(fm(np.concatenate([k, k[:, perm]], axis=1)))
    z7 = np.zeros((1024, 7), np.float32)
    for h in range(4):
        out.append(fm(np.concatenate([cols(IN_OFF["fk"], 256 * g + 64 * h, 64), z7], axis=1)))
    for h in range(4):
        out.append(fm(np.concatenate([cols(IN_OFF["fq"], 256 * g + 64 * h, 64), z7], axis=1)))
    order = np.concatenate([np.concatenate([np.arange(256 * gg, 256 * gg + 256), 512 + np.arange(256 * gg, 256 * gg + 256)])
                            for gg in range(2)])
    wo = w_out[order]
    for kc in range(8):
        out.append(np.ascontiguousarray(wo[kc * 128:(kc + 1) * 128]))
    flat = np.concatenate([p.reshape(-1) for p in out]).astype(np.float32)
    assert flat.size == W_TOTAL, (flat.size, W_TOTAL)
    return flat.reshape(W_TOTAL // 1024, 1024)


_NC_CACHE = {}


def kernel(x, c, positions, w_ada, b_ada, g_pre, w_in, b_fgate, sinks, w_out, g_post):
    x = np.asarray(x, np.float32)
    c = np.asarray(c, np.float32)
    positions = np.asarray(positions, np.int32)
    w_ada = np.ascontiguousarray(np.asarray(w_ada, np.float32)[0])
    b_ada = np.asarray(b_ada, np.float32)[0]
    g_pre = np.asarray(g_pre, np.float32)[0]
    w_in = np.asarray(w_in, np.float32)[0]
    b_fgate = np.asarray(b_fgate, np.float32)[0]
    sinks = np.asarray(sinks, np.float32)[0]
    w_out = np.asarray(w_out, np.float32)[0]
    g_post = np.asarray(g_post, np.float32)[0]

    if "nc" not in _NC_CACHE:
        _NC_CACHE["nc"] = build_nc()
    nc = _NC_CACHE["nc"]

    half = HD // 2
    inv_freq = (10000.0 ** (-np.arange(half, dtype=np.float32) / half)).astype(np.float32)
    invf = (np.tile(inv_freq, 4).astype(np.float64) / (2.0 * math.pi)).reshape(128, 1).astype(np.float32)
    phpi = np.concatenate([np.full(64, 0.25), np.full(32, 0.5), np.zeros(32)]).astype(np.float32).reshape(128, 1)
    wst = [_pieces_for_group(w_in, w_out, g) for g in range(2)]
    w_ada_r = np.ascontiguousarray(w_ada.reshape(8, 128, 24, 128).transpose(2, 1, 0, 3)).reshape(24 * 128, 1024)
    in_maps = []
    for core in range(8):
        b, g = core // 2, core % 2
        in_maps.append({
            "x": np.ascontiguousarray(x[b]),
            "cvec": np.ascontiguousarray(c[b].reshape(8, 128).T),
            "pos": np.ascontiguousarray(np.broadcast_to(positions[b][None, :], (128, S))),
            "w_ada": w_ada_r,
            "b_adaT": np.ascontiguousarray(b_ada.reshape(24, 128).T),
            "b_gate_rep": np.ascontiguousarray(np.broadcast_to(b_ada[2048:3072][None, :], (128, D))),
            "g_preT": np.ascontiguousarray(g_pre.reshape(8, 128).T),
            "g_post_rep": np.ascontiguousarray(np.broadcast_to(g_post[None, :], (128, D))),
            "wstream": wst[g],
            "bfg_rep": np.ascontiguousarray(np.broadcast_to(b_fgate[4 * g:4 * g + 4][None, :], (128, 4))),
            "sink_rep": np.ascontiguousarray(np.broadcast_to(sinks[4 * g:4 * g + 4][None, :], (128, 4))),
            "invf": invf,
            "phpi": phpi,
        })
    res = run_bass_kernel_spmd(nc, in_maps, core_ids=list(range(8)))
    out = np.stack([np.asarray(res.results[2 * b]["y"], np.float32) for b in range(4)], axis=0)
    return out
```
